# Optimizing a Trainium2 kernel written in Bass

```python
import math
import jax, jax.numpy as jnp
from jax import lax
import numpy as np

D_MODEL = 2048
BATCH = 2
SEQ = 16384
DEPTH = 2

HEAD_DIM = 64
N_Q_HEADS = D_MODEL // HEAD_DIM
N_KV_HEADS = N_Q_HEADS // 8
Q_PER_KV = N_Q_HEADS // N_KV_HEADS
WINDOW = 128
BLOCK = 128
QKV_WIDTH = (N_Q_HEADS + 2 * N_KV_HEADS) * HEAD_DIM
NUM_BUCKETS = 32
MAX_DISTANCE = 128
GMLP_CHUNK = 128
GMLP_INNER = D_MODEL
GMLP_GROUPS = 16
GMLP_GROUP_DIM = GMLP_INNER // GMLP_GROUPS
FFN_HIDDEN = -(-8 * D_MODEL // (3 * 256)) * 256
N_MIXERS = 2
N_ATTN_LAYERS = (DEPTH + 1) // 2
N_GMLP_LAYERS = DEPTH // 2
NORM_EPS = 1e-6
LN_EPS = 1e-5
NEG_INF = -1e30

kernel_name = "hybrid_swa_sink_gmlp_sandwich"


def rmsnorm(x, gain):
    xf = x.astype(jnp.float32)
    y = xf * lax.rsqrt(jnp.mean(xf * xf, axis=-1, keepdims=True) + NORM_EPS)
    return (y * gain.astype(jnp.float32)).astype(x.dtype)


def layernorm(x, gain, bias):
    xf = x.astype(jnp.float32)
    mu = jnp.mean(xf, axis=-1, keepdims=True)
    xc = xf - mu
    y = xc * lax.rsqrt(jnp.mean(xc * xc, axis=-1, keepdims=True) + LN_EPS)
    return (y * gain.astype(jnp.float32) + bias.astype(jnp.float32)).astype(x.dtype)


def t5_causal_bucket(dist):
    max_exact = NUM_BUCKETS // 2
    is_small = dist < max_exact
    d_f = jnp.maximum(dist, 1).astype(jnp.float32)
    large = max_exact + (jnp.log(d_f / max_exact) / math.log(MAX_DISTANCE / max_exact)
                         * (NUM_BUCKETS - max_exact)).astype(jnp.int32)
    large = jnp.minimum(large, NUM_BUCKETS - 1)
    return jnp.where(is_small, dist, large)


def swa_sink_attention(h, w_qkv, w_o, sinks, rel_bias_table):
    B, S, _ = h.shape
    nb = S // BLOCK
    qkv = h @ w_qkv
    q_w = N_Q_HEADS * HEAD_DIM
    kv_w = N_KV_HEADS * HEAD_DIM
    q = qkv[..., :q_w].reshape(B, nb, BLOCK, N_KV_HEADS, Q_PER_KV, HEAD_DIM)
    k = qkv[..., q_w:q_w + kv_w].reshape(B, S, N_KV_HEADS, HEAD_DIM)
    v = qkv[..., q_w + kv_w:].reshape(B, S, N_KV_HEADS, HEAD_DIM)

    def band(t):
        prev = jnp.pad(t, ((0, 0), (BLOCK, 0), (0, 0), (0, 0)))[:, :S]
        return jnp.concatenate([prev.reshape(B, nb, BLOCK, N_KV_HEADS, HEAD_DIM),
                                t.reshape(B, nb, BLOCK, N_KV_HEADS, HEAD_DIM)], axis=2)

    kb, vb = band(k), band(v)
    scale = HEAD_DIM ** -0.5
    s = jnp.einsum('bnqkgd,bnjkd->bnkgqj', q, kb).astype(jnp.float32) * scale

    qi = jnp.arange(BLOCK, dtype=jnp.int32)[:, None] + BLOCK
    kj = jnp.arange(2 * BLOCK, dtype=jnp.int32)[None, :]
    dist = qi - kj
    bucket = t5_causal_bucket(jnp.maximum(dist, 0))
    bias = jnp.transpose(rel_bias_table[bucket].astype(jnp.float32), (2, 0, 1))
    bias = bias.reshape(N_KV_HEADS, Q_PER_KV, BLOCK, 2 * BLOCK)
    in_window = (dist >= 0) & (dist < WINDOW)
    blk = jnp.arange(nb, dtype=jnp.int32)[:, None, None]
    key_exists = (blk * BLOCK + kj[None] - BLOCK) >= 0
    mask = in_window[None] & key_exists
    s = jnp.where(mask[None, :, None, None], s + bias[None, None], NEG_INF)

    sink = sinks.astype(jnp.float32).reshape(1, 1, N_KV_HEADS, Q_PER_KV, 1, 1)
    m = jnp.maximum(jnp.max(s, axis=-1, keepdims=True), sink)
    p = jnp.exp(s - m)
    probs = p / (jnp.sum(p, axis=-1, keepdims=True) + jnp.exp(sink - m))
    o = jnp.einsum('bnkgqj,bnjkd->bnqkgd', probs.astype(vb.dtype), vb)
    return o.reshape(B, S, q_w) @ w_o


def chunked_gmlp(h, w_in, ln_gain, ln_bias, w_spatial, b_spatial, w_out):
    B, S, _ = h.shape
    nc = S // GMLP_CHUNK
    z = jax.nn.gelu(h @ w_in)
    u, v = z[..., :GMLP_INNER], z[..., GMLP_INNER:]
    v = layernorm(v, ln_gain, ln_bias)
    v = v.reshape(B, nc, GMLP_CHUNK, GMLP_GROUPS, GMLP_GROUP_DIM)
    causal = jnp.tril(jnp.ones((GMLP_CHUNK, GMLP_CHUNK), dtype=bool))
    w_s = jnp.where(causal[None], w_spatial, jnp.zeros_like(w_spatial))
    mixed = jnp.einsum('gts,bnsgc->bntgc', w_s, v) + b_spatial.T[:, :, None]
    gated = u * mixed.reshape(B, S, GMLP_INNER)
    return gated @ w_out


def swiglu(h, w_gate_up, w_down):
    gu = h @ w_gate_up
    g, up = gu[..., :FFN_HIDDEN], gu[..., FFN_HIDDEN:]
    return (jax.nn.silu(g) * up) @ w_down


def setup_inputs(seed: int = 0) -> dict:
    key = jax.random.key(seed)
    ks = jax.random.split(key, 16)
    f32 = jnp.float32

    def dense(k, shape, fan_in):
        return jax.random.normal(k, shape, f32) * fan_in ** -0.5

    x = jax.random.normal(ks[0], (BATCH, SEQ, D_MODEL), f32)
    attn_w_qkv = dense(ks[1], (N_ATTN_LAYERS, D_MODEL, QKV_WIDTH), D_MODEL)
    attn_w_o = dense(ks[2], (N_ATTN_LAYERS, N_Q_HEADS * HEAD_DIM, D_MODEL), N_Q_HEADS * HEAD_DIM)
    attn_sinks = jax.random.normal(ks[3], (N_ATTN_LAYERS, N_Q_HEADS), f32)
    rel_bias_table = 0.5 * jax.random.normal(ks[4], (NUM_BUCKETS, N_Q_HEADS), f32)
    gmlp_w_in = dense(ks[5], (N_GMLP_LAYERS, D_MODEL, 2 * GMLP_INNER), D_MODEL)
    gmlp_ln_gain = 1.0 + 0.1 * jax.random.normal(ks[6], (N_GMLP_LAYERS, GMLP_INNER), f32)
    gmlp_ln_bias = 0.1 * jax.random.normal(ks[7], (N_GMLP_LAYERS, GMLP_INNER), f32)
    gmlp_w_spatial = dense(ks[8], (N_GMLP_LAYERS, GMLP_GROUPS, GMLP_CHUNK, GMLP_CHUNK), GMLP_CHUNK)
    gmlp_b_spatial = 1.0 + 0.1 * jax.random.normal(ks[9], (N_GMLP_LAYERS, GMLP_GROUPS, GMLP_CHUNK), f32)
    gmlp_w_out = dense(ks[10], (N_GMLP_LAYERS, GMLP_INNER, D_MODEL), GMLP_INNER)
    norm_gains = 1.0 + 0.1 * jax.random.normal(ks[11], (DEPTH, 4, D_MODEL), f32)
    ffn_w_gate_up = dense(ks[12], (DEPTH, D_MODEL, 2 * FFN_HIDDEN), D_MODEL)
    ffn_w_down = dense(ks[13], (DEPTH, FFN_HIDDEN, D_MODEL), FFN_HIDDEN)
    return {"x": x, "attn_w_qkv": attn_w_qkv, "attn_w_o": attn_w_o, "attn_sinks": attn_sinks,
            "rel_bias_table": rel_bias_table, "gmlp_w_in": gmlp_w_in, "gmlp_ln_gain": gmlp_ln_gain,
            "gmlp_ln_bias": gmlp_ln_bias, "gmlp_w_spatial": gmlp_w_spatial,
            "gmlp_b_spatial": gmlp_b_spatial, "gmlp_w_out": gmlp_w_out, "norm_gains": norm_gains,
            "ffn_w_gate_up": ffn_w_gate_up, "ffn_w_down": ffn_w_down}


def reference(x, attn_w_qkv, attn_w_o, attn_sinks, rel_bias_table, gmlp_w_in, gmlp_ln_gain,
              gmlp_ln_bias, gmlp_w_spatial, gmlp_b_spatial, gmlp_w_out, norm_gains,
              ffn_w_gate_up, ffn_w_down):
    for i in range(DEPTH):
        h = rmsnorm(x, norm_gains[i, 0])
        j = i // N_MIXERS
        if i % N_MIXERS == 0:
            mix = swa_sink_attention(h, attn_w_qkv[j], attn_w_o[j], attn_sinks[j], rel_bias_table)
        else:
            mix = chunked_gmlp(h, gmlp_w_in[j], gmlp_ln_gain[j], gmlp_ln_bias[j],
                               gmlp_w_spatial[j], gmlp_b_spatial[j], gmlp_w_out[j])
        x = x + rmsnorm(mix, norm_gains[i, 1])
        h = rmsnorm(x, norm_gains[i, 2])
        x = x + rmsnorm(swiglu(h, ffn_w_gate_up[i], ffn_w_down[i]), norm_gains[i, 3])
    return x
```

```python
import contextlib
import math

import numpy as np

import concourse.bass as bass
import concourse.mybir as mybir
from concourse.bass_utils import run_bass_kernel_spmd

F32 = mybir.dt.float32
BF16 = mybir.dt.bfloat16
AF = mybir.ActivationFunctionType
ALU = mybir.AluOpType

N_CORES = 8
D = 2048
KC = 16
T = 512
NB = 4
HC = 44
HALO = 128
NSLOT = 4
SLOT_E = 5632
NORM_EPS = 1e-6
LN_EPS = 1e-5
NEG = -1e30
SAME_ENGINE_SYNC = True
NCONV = 12
CONV_LOOK = 20

A0_Q, A0_KD, A0_V, A0_WO, A0_GU = 0, 8, 10, 11, 19
A1_U, A1_V, A1_WOUT, A1_GU = 0, 8, 16, 24
NA = (63, 68)


class Res:
    __slots__ = ("name", "w", "r", "guard")

    def __init__(self, name):
        self.name = name
        self.w = None
        self.r = {}
        self.guard = {}


class Sched:
    def __init__(self, nc, stack):
        self.nc = nc
        self.stack = stack
        self.engs = ["pe", "act", "dve", "pool", "sp"]
        self.sems = {}
        self.cnt = {}
        self.waited = {e: {} for e in self.engs}
        self.streams = {e: [] for e in self.engs}
        for e in self.engs:
            self.new_sem(e)

    def new_sem(self, key):
        self.sems[key] = self.stack.enter_context(self.nc.semaphore("s_" + key))
        self.cnt[key] = 0

    def op(self, eng, fn, reads=(), writes=(), sem=None, inc=1):
        deps = {}

        def add(k, v):
            if deps.get(k, 0) < v:
                deps[k] = v

        for r in reads:
            if r.w is not None:
                add(*r.w)
        for w in writes:
            if w.w is not None:
                add(*w.w)
            for k, v in w.r.items():
                add(k, v)
            for k, v in w.guard.items():
                add(k, v)
            w.guard = {}
        if eng == "pe" or not SAME_ENGINE_SYNC:
            deps.pop(eng, None)
        st = self.streams[eng]
        wd = self.waited[eng]
        for k, v in deps.items():
            if wd.get(k, 0) < v:
                wd[k] = v
                st.append(("wait", k, v))
        key = sem or eng
        self.cnt[key] += inc
        val = self.cnt[key]
        st.append(("op", fn, key, inc))
        for r in reads:
            if r.r.get(key, 0) < val:
                r.r[key] = val
        for w in writes:
            w.w = (key, val)
            w.r = {}
        return (key, val)

    def snapshot(self, res_list):
        deps = {}
        for r in res_list:
            if r.w is not None and deps.get(r.w[0], 0) < r.w[1]:
                deps[r.w[0]] = r.w[1]
            for k, v in r.r.items():
                if deps.get(k, 0) < v:
                    deps[k] = v
        return deps

    def guard(self, new_list, old_list):
        deps = self.snapshot(old_list)
        for r in new_list:
            for k, v in deps.items():
                if r.guard.get(k, 0) < v:
                    r.guard[k] = v

    def barrier(self, keys):
        for e in self.engs:
            for k in keys:
                v = self.cnt[k]
                if v > 0 and self.waited[e].get(k, 0) < v and k != e:
                    self.waited[e][k] = v
                    self.streams[e].append(("wait", k, v))

    def final_wait(self, eng, keys):
        for k in keys:
            self.streams[eng].append(("wait", k, self.cnt[k]))

    def replay(self, eng, e):
        for item in self.streams[eng]:
            if item[0] == "wait":
                e.wait_ge(self.sems[item[1]], item[2])
            else:
                _, fn, key, inc = item
                ins = fn(e)
                ins.then_inc(self.sems[key], inc)


def build_program(n_tiles, layers=(0, 1)):
    nc = bass.Bass("TRN2", target_bir_lowering=False)
    tpc = n_tiles * T

    def din(name, shape, dt=F32):
        return nc.dram_tensor(name, list(shape), dt, kind="ExternalInput").ap()

    x_in = din("x_in", [KC, 128, tpc])
    xh_in = din("xh_in", [KC, 128, HALO])
    out_d = nc.dram_tensor("out", [KC, 128, tpc], F32, kind="ExternalOutput").ap()
    wA = [din("wA0", [NA[0], 128, 4096]), din("wA1", [NA[1], 128, 4096])]
    wD = [din("wD0", [16, 128, SLOT_E]), din("wD1", [16, 128, SLOT_E])]
    wAb = [nc.dram_tensor("wA%db" % l, [NA[l], 128, 4096], BF16, kind="Internal").ap() for l in range(2)]
    wDb = [nc.dram_tensor("wD%db" % l, [16, 128, SLOT_E], BF16, kind="Internal").ap() for l in range(2)]
    gcol_d = din("gcol", [128, 8 * KC])
    biasC_d = din("biasC", [128, 32 * 128])
    sinkb_d = din("sinkb", [128, 32])
    haloneg_d = din("haloneg", [128, 1])
    lncol_d = din("lncol", [128, 2 * KC])
    bsp_d = din("bspb", [128, KC * 128])
    wst_d = din("wst", [128, KC * 128])

    with contextlib.ExitStack() as stack:
        S = Sched(nc, stack)

        def sb(name, shape, dt):
            return stack.enter_context(nc.sbuf_tensor("sb_" + name, list(shape), dt))

        def ps(name):
            return stack.enter_context(nc.psum_tensor(name, [128, 512], F32))

        xT = sb("xT", [128, KC, T], F32)
        AY = sb("AY", [128, KC * T], F32)
        Bar = sb("Bar", [128, 12288], F32)
        kdT = sb("kdT", [128, 4, HALO + T], BF16)
        VV = sb("VV", [128, NB + 1, 4, 128], BF16)
        biasC = sb("biasC", [128, 8, 4, 128], F32)
        Amat = sb("Amat", [128, KC, 128], F32)
        WsT = sb("WsT", [128, KC, 128], BF16)
        gcol = sb("gcol", [128, 8, KC], F32)
        lncol = sb("lncol", [128, 2, KC], F32)
        sinkb = sb("sinkb", [128, 32], F32)
        expsink = sb("expsink", [128, 32], F32)
        haloneg = sb("haloneg", [128, 1], F32)
        mmat = sb("mmat", [128, 128], BF16)
        ones_bf = sb("ones_bf", [128, 128], BF16)
        ones_f = sb("ones_f", [128, 128], F32)
        wring = sb("wring", [128, NSLOT, SLOT_E], BF16)
        sqring = sb("sqring", [128, 3, T], BF16)
        rstd = sb("rstd", [128, 2, T], F32)
        lnst = sb("lnst", [128, NB, 8, 6], F32)
        lnmv = sb("lnmv", [128, NB, 4], F32)
        ident = sb("ident", [128, 128], BF16)
        maskT = sb("maskT", [128, 2, T], BF16)
        sinkhl = sb("sinkhl", [128, 32], BF16)
        sinktmp = sb("sinktmp", [128, 2, 32], F32)
        m01 = sb("m01", [128, 2], F32)
        epsc = sb("epsc", [128, 2], F32)
        inv128 = sb("inv128", [128, 1], F32)
        rtok = sb("rtok", [128, NB], F32)
        print("sbuf bytes remaining", nc.sbuf_bytes_remaining)

        AYb = AY[:].bitcast(BF16)
        hT = AYb[:, 0:KC * (HALO + T)].rearrange("p (c n) -> p c n", c=KC)
        Y = AY[:].rearrange("p (c n) -> p c n", c=KC)
        vn = AYb[:, 0:NB * D].rearrange("p (b f) -> p b f", b=NB)
        wsf = AY[:, 0:KC * 128].rearrange("p (g t) -> p g t", g=KC)
        utmp = AY[:, 6144:6144 + 2 * T].rearrange("p (k n) -> p k n", k=2)
        tmpg = AY[:, 5120:5120 + 2 * T].rearrange("p (k n) -> p k n", k=2)
        Bb = Bar[:].bitcast(BF16)
        qT = Bb[:, 0:KC * T].rearrange("p (c n) -> p c n", c=KC)
        tmpr = Bar[:, 4096:4096 + 4 * T].rearrange("p (k n) -> p k n", k=4)
        ptr = Bb[:, 12288:12288 + 4 * T].rearrange("p (k n) -> p k n", k=4)
        denr = Bar[:, 7168:7168 + 2 * T].rearrange("p (k n) -> p k n", k=2)
        xh = Bar[:, 10240:10240 + KC * HALO].rearrange("p (c n) -> p c n", c=KC)
        uT = qT
        vtok = Bar[:, 4096:4096 + NB * D].rearrange("p (b f) -> p b f", b=NB)
        hid = Bb[:, 0:HC * T].rearrange("p (j n) -> p j n", j=HC)

        def vnb(blk):
            if blk == 0:
                return AYb[:, 14336:14336 + D]
            o = 8192 + (blk - 1) * D
            return Bb[:, o:o + D]
        sgr = Bar[:, 11264:11264 + 2 * T].rearrange("p (k n) -> p k n", k=2)

        psb = [ps("ps%d" % i) for i in range(8)]

        def RL(name, n):
            return [Res("%s%d" % (name, i)) for i in range(n)]

        r_x = RL("x", KC)
        r_h = RL("h", KC)
        r_Y = RL("Y", KC)
        r_vn = []
        r_vn0 = Res("vn0")
        r_q = RL("q", KC)
        r_tmp = RL("tmp", 4)
        r_pt = RL("pt", 4)
        r_den = RL("den", 2)
        r_xh = RL("xh", 1)
        r_vtok = RL("vtok", NB)
        r_hid = RL("hid", HC)
        r_sg = RL("sg", 2)
        r_kd = RL("kd", 4)
        r_vv = RL("vv", NB + 1)
        r_ps = RL("ps", 6)
        r_ss = RL("ss", 2)
        r_sq = RL("sq", 3)
        r_rstd = RL("rstd", 2)
        r_ws = RL("ws", NSLOT)
        r_ln = RL("ln", NB)
        r_setup = Res("setup")
        r_tmpg = RL("tmpg", 2)
        r_rtok = Res("rtok")
        r_utmp = RL("utmp", 2)
        r_scrA = [RL("scrA0_", NA[0]), RL("scrA1_", NA[1])]
        r_scrD = [RL("scrD0_", 16), RL("scrD1_", 16)]
        r_conv = RL("conv", NCONV)
        for i in range(NSLOT):
            S.new_sem("wslot%d" % i)
        for i in range(NSLOT):
            S.new_sem("wcast%d" % i)
            S.new_sem("wstore%d" % i)
        S.new_sem("setupdma")
        S.new_sem("xhld")
        for c in range(KC):
            S.new_sem("xld%d" % c)
            S.new_sem("xlp%d" % c)
            S.new_sem("ost%d" % c)

        AY_groups = [r_h, r_Y, r_vn]
        B_attn = r_q + r_tmp + r_pt + r_den + r_xh
        B_gmlp = r_q + r_vtok
        B_ffn = r_hid + r_sg

        state = {"ps": 0, "ss": 0, "sq": 0, "rstd": 0, "tmp": 0, "pt": 0, "den": 0, "sg": 0,
                 "ev": 0, "tmpg": 0, "utmp": 0}

        def nxt(name, n):
            v = state[name]
            state[name] = (v + 1) % n
            return v

        pieces = []
        for t in range(n_tiles):
            for l in range(2):
                if l not in layers:
                    continue
                for i in range(NA[l]):
                    pieces.append((wAb[l][i], 4096, r_scrA[l][i], wA[l][i], t))
                    if i == NA[l] - 1:
                        for c in range(16):
                            pieces.append((wDb[l][c], SLOT_E, r_scrD[l][c], wD[l][c], t))
        wstate = {"next_load": 0, "next_use": 0}

        def issue_load():
            i = wstate["next_load"]
            if i >= len(pieces):
                return
            wstate["next_load"] = i + 1
            scr, ne, res, src32, t = pieces[i]
            s = i % NSLOT
            if t == 0:
                S.op("pool", lambda e: e.dma_start(out=wring[:, s, 0:ne], in_=src32),
                     reads=[], writes=[r_ws[s]], sem="wcast%d" % s, inc=16)
                if n_tiles > 1:
                    S.op("sp", lambda e: e.dma_start(out=scr, in_=wring[:, s, 0:ne]),
                         reads=[r_ws[s]], writes=[res], sem="wstore%d" % s, inc=16)
            else:
                S.op("sp", lambda e: e.dma_start(out=wring[:, s, 0:ne], in_=scr),
                     reads=[res], writes=[r_ws[s]], sem="wslot%d" % s, inc=16)

        def next_piece():
            i = wstate["next_use"]
            wstate["next_use"] = i + 1
            return i % NSLOT

        def done_piece():
            issue_load()

        def sdma(dst, src):
            S.op("sp", lambda e: e.dma_start(out=dst, in_=src), writes=[r_setup], sem="setupdma", inc=16)

        sdma(gcol[:].rearrange("p a c -> p (a c)"), gcol_d)
        sdma(biasC[:].rearrange("p a i q -> p (a i q)"), biasC_d)
        sdma(sinkb[:], sinkb_d)
        sdma(haloneg[:], haloneg_d)
        sdma(lncol[:].rearrange("p a c -> p (a c)"), lncol_d)
        sdma(Amat[:].rearrange("p g t -> p (g t)"), bsp_d)
        sdma(AY[:, 0:KC * 128], wst_d)
        r_c = [Res("c%d" % i) for i in range(8)]
        S.op("pool", lambda e: e.memset(mmat[:], 1.0 / D), writes=[r_c[0]])
        S.op("pool", lambda e: e.memset(ones_bf[:], 1.0), writes=[r_c[1]])
        S.op("pool", lambda e: e.memset(ones_f[:], 1.0), writes=[r_c[2]])
        S.op("pool", lambda e: e.memset(epsc[:, 0:1], NORM_EPS), writes=[r_c[7]])
        S.op("pool", lambda e: e.memset(epsc[:, 1:2], LN_EPS), writes=[r_c[7]])
        S.op("pool", lambda e: e.memset(inv128[:], 1.0 / 128), writes=[r_c[7]])
        S.op("pool", lambda e: e.affine_select(out=wsf, in_=wsf, pattern=[[0, KC], [1, 128]],
                                               compare_op=ALU.is_ge, fill=0.0, base=0,
                                               channel_multiplier=-1),
             reads=[r_setup], writes=[r_c[3]])
        S.op("act", lambda e: e.copy(out=WsT[:], in_=wsf), reads=[r_c[3]], writes=[r_c[4]])
        S.op("act", lambda e: e.activation(out=expsink[:], in_=sinkb[:], func=AF.Exp),
             reads=[r_setup], writes=[r_c[5]])
        S.op("pool", lambda e: e.memset(m01[:], 1.0), writes=[r_c[5]])
        S.op("pool", lambda e: e.affine_select(out=m01[:, 0:1], in_=m01[:, 0:1], pattern=[[0, 1]], compare_op=ALU.is_equal,
                                               fill=0.0, base=0, channel_multiplier=1), writes=[r_c[5]])
        S.op("pool", lambda e: e.affine_select(out=m01[:, 1:2], in_=m01[:, 1:2], pattern=[[0, 1]], compare_op=ALU.is_equal,
                                               fill=0.0, base=-1, channel_multiplier=1), writes=[r_c[5]])
        S.op("dve", lambda e: e.tensor_copy(out=sinkhl[:], in_=expsink[:]), reads=[r_c[5]], writes=[r_c[5]])
        S.op("dve", lambda e: e.tensor_copy(out=sinktmp[:, 0, :], in_=sinkhl[:]), reads=[], writes=[r_c[5]])
        S.op("dve", lambda e: e.tensor_tensor(out=sinktmp[:, 1, :], in0=expsink[:], in1=sinktmp[:, 0, :],
                                              op=ALU.subtract), reads=[], writes=[r_c[5]])
        S.op("dve", lambda e: e.tensor_scalar(out=sinktmp[:, 0, :], in0=sinktmp[:, 0, :], scalar1=m01[:, 0:1],
                                              scalar2=None, op0=ALU.mult), reads=[], writes=[r_c[5]])
        S.op("dve", lambda e: e.scalar_tensor_tensor(out=sinktmp[:, 0, :], in0=sinktmp[:, 1, :], scalar=m01[:, 1:2],
                                                     in1=sinktmp[:, 0, :], op0=ALU.mult, op1=ALU.add),
             reads=[], writes=[r_c[5]])
        S.op("dve", lambda e: e.tensor_copy(out=sinkhl[:], in_=sinktmp[:, 0, :]), reads=[], writes=[r_c[5]])
        S.op("pool", lambda e: e.memset(ident[:], 1.0), writes=[r_c[7]])
        S.op("pool", lambda e: e.affine_select(out=ident[:], in_=ident[:], pattern=[[-1, 128]], compare_op=ALU.is_equal,
                                               fill=0.0, base=0, channel_multiplier=1), writes=[r_c[7]])
        S.op("pool", lambda e: e.memset(maskT[:], 0.0), writes=[r_c[7]])
        mp = maskT[:, 0, :].rearrange("p (i q) -> p i q", i=4)
        mc = maskT[:, 1, :].rearrange("p (i q) -> p i q", i=4)
        S.op("pool", lambda e: e.affine_select(out=mp, in_=mp, pattern=[[0, 4], [-1, 128]], compare_op=ALU.is_gt,
                                               fill=NEG, base=0, channel_multiplier=1), writes=[r_c[7]])
        S.op("pool", lambda e: e.affine_select(out=mc, in_=mc, pattern=[[0, 4], [1, 128]], compare_op=ALU.is_ge,
                                               fill=NEG, base=0, channel_multiplier=-1), writes=[r_c[7]])
        for _ in range(NSLOT):
            issue_load()
        for k in range(4):
            S.op("pe", lambda e, k=k: e.matmul(psb[k][:].rearrange("p (a t) -> p a t", a=4), lhsT=ones_f[:], rhs=wsf[:, 4 * k:4 * k + 4, :],
                                               start=True, stop=True),
                 reads=[r_c[2], r_c[3]], writes=[r_ps[k]])
            for gg in range(4):
                g = 4 * k + gg
                S.op("dve", lambda e, k=k, gg=gg, g=g: e.scalar_tensor_tensor(
                    out=Amat[:, g, :], in0=psb[k][:, gg * 128:(gg + 1) * 128], scalar=lncol[:, 1, g:g + 1],
                    in1=Amat[:, g, :], op0=ALU.mult, op1=ALU.add),
                    reads=[r_ps[k], r_setup], writes=[r_c[6]])
        S.barrier(["pe", "act", "dve", "pool", "setupdma"])

        def evac_engine():
            v = state["ev"]
            state["ev"] = v ^ 1
            return "act" if v == 0 else "dve"

        def proj_fm(w_ap_fn, rhs_fn, rhs_res, nk, n, evac, M=128):
            b = nxt("ps", 6)

            def fn(e):
                ins = None
                for k in range(nk):
                    ins = e.matmul(psb[b][0:M, 0:n], lhsT=w_ap_fn(k), rhs=rhs_fn(k),
                                   start=(k == 0), stop=(k == nk - 1))
                return ins
            return b, fn

        def proj_pair_kouter(s, wfn_a, wfn_b):
            ba = nxt("ps", 6)
            bb = nxt("ps", 6)
            for k in range(KC):
                def fn(e, k=k):
                    e.matmul(psb[ba][:], lhsT=wfn_a(k), rhs=hT[:, k, HALO:HALO + T], start=(k == 0), stop=(k == KC - 1))
                    return e.matmul(psb[bb][:], lhsT=wfn_b(k), rhs=hT[:, k, HALO:HALO + T], start=(k == 0),
                                    stop=(k == KC - 1))
                S.op("pe", fn, reads=[r_ws[s], r_h[k]], writes=[r_ps[ba], r_ps[bb]])
            return ba, bb

        def stats_begin():
            return nxt("ss", 2)

        def stats_sq(src_ap, src_res, n=T):
            k = nxt("sq", 3)
            S.op("act", lambda e: e.activation(out=sqring[:, k, 0:n], in_=src_ap, func=AF.Square),
                 reads=src_res, writes=[r_sq[k]])
            return k

        def stats_mm(ssb, k, first, last, n=T):
            S.op("pe", lambda e: e.matmul(psb[6 + ssb][:, 0:n], lhsT=mmat[:], rhs=sqring[:, k, 0:n],
                                          start=first, stop=last),
                 reads=[r_sq[k]], writes=[r_ss[ssb]])

        def stats_add(ssb, src_ap, src_res, first, last, n=T):
            k = stats_sq(src_ap, src_res, n)
            stats_mm(ssb, k, first, last, n)

        def stats_finish(ssb, n=T):
            k = nxt("rstd", 2)
            S.op("act", lambda e: e.activation(out=rstd[:, k, 0:n], in_=psb[6 + ssb][:, 0:n], func=AF.Ln,
                                               bias=epsc[:, 0:1], scale=1.0),
                 reads=[r_ss[ssb]], writes=[r_rstd[k]])
            S.op("act", lambda e: e.activation(out=rstd[:, k, 0:n], in_=rstd[:, k, 0:n], func=AF.Exp, scale=-0.5),
                 reads=[], writes=[r_rstd[k]])
            return k

        def prenorm(gidx, guard_old, tokmajor=False):
            others = [r for r in guard_old if r not in r_Y]
            for c in range(KC):
                j0 = (c * 1280) // 2048
                j1 = ((c + 1) * 1280 - 1) // 2048
                ys = r_Y[j0:j1 + 1] if any(r in r_Y for r in guard_old) else []
                S.guard([r_h[c]], ys + others)
            ssb = stats_begin()
            for c in range(KC):
                stats_add(ssb, xT[:, c, :], [r_x[c]], c == 0, c == KC - 1)
                S.op("act", lambda e, c=c: e.mul(out=hT[:, c, HALO:HALO + T], in_=xT[:, c, :],
                                                 mul=gcol[:, gidx, c:c + 1]),
                     reads=[r_x[c]], writes=[r_h[c]])
            rk = stats_finish(ssb)
            if tokmajor:
                b = nxt("ps", 6)

                def ft(e):
                    ins = None
                    for blk in range(NB):
                        ins = e.matmul(psb[b][:, blk:blk + 1], lhsT=rstd[:, rk, blk * 128:(blk + 1) * 128],
                                       rhs=inv128[:, 0:1], start=True, stop=True)
                    return ins
                S.op("pe", ft, reads=[r_rstd[rk]], writes=[r_ps[b]])
                S.op("act", lambda e: e.copy(out=rtok[:, :], in_=psb[b][:, 0:NB]), reads=[r_ps[b]], writes=[r_rtok])
            return rk

        def y_evac(b, c):
            S.op("act", lambda e: e.copy(out=Y[:, c, :], in_=psb[b][:]), reads=[r_ps[b]], writes=[r_Y[c]])
            return stats_sq(psb[b][:], [r_ps[b]])

        def postnorm_residual(gidx, ssb, after_chunk=None):
            rk = stats_finish(ssb)
            for c in range(KC):
                S.op("dve", lambda e, c=c: e.scalar_tensor_tensor(
                    out=Y[:, c, :], in0=Y[:, c, :], scalar=gcol[:, gidx, c:c + 1],
                    in1=rstd[:, rk, :], op0=ALU.mult, op1=ALU.mult),
                    reads=[r_rstd[rk]], writes=[r_Y[c]])
                S.op("dve", lambda e, c=c: e.tensor_tensor(out=xT[:, c, :], in0=xT[:, c, :], in1=Y[:, c, :],
                                                           op=ALU.add),
                     reads=[r_Y[c]], writes=[r_x[c]])
                if after_chunk is not None:
                    after_chunk(c)

        def out_proj(rhs_ap, rhs_res, nk, piece_w, guard_old):
            S.guard(r_Y, guard_old)
            ssb = stats_begin()
            pend = None
            for c in range(KC):
                if nk == KC:
                    if c % 2 == 0:
                        s = next_piece()
                    wv = wring[:, s, 0:4096].rearrange("p (k m) -> p k m", k=KC)
                    m0 = (c % 2) * 128
                    wfn = lambda k, wv=wv, m0=m0: wv[:, k, m0:m0 + 128]
                else:
                    s = next_piece()
                    wv = wring[:, s, 0:SLOT_E].rearrange("p (k m) -> p k m", k=HC)
                    wfn = lambda k, wv=wv: wv[:, k, :]
                b, fn = proj_fm(wfn, lambda k: rhs_ap[:, k, :], None, nk, T, None)
                S.op("pe", fn, reads=[r_ws[s]] + rhs_res, writes=[r_ps[b]])
                if nk != KC or c % 2 == 1:
                    done_piece()
                if pend is not None:
                    stats_mm(ssb, pend[0], pend[1] == 0, False)
                pend = (y_evac(b, c), c)
            stats_mm(ssb, pend[0], False, True)
            return ssb

        def ffn(l, gpre, gpost, after_chunk=None):
            rk = prenorm(gpre, r_Y + r_vn)
            S.guard(B_ffn, B_attn + B_gmlp)
            for j in range(HC):
                s = next_piece()
                wv = wring[:, s, 0:4096].rearrange("p (k m) -> p k m", k=KC)
                if j == 0:
                    bg, bu = proj_pair_kouter(s, lambda k, wv=wv: wv[:, k, 0:128], lambda k, wv=wv: wv[:, k, 128:256])
                else:
                    bg, fg = proj_fm(lambda k, wv=wv: wv[:, k, 0:128], lambda k: hT[:, k, HALO:HALO + T], None, KC, T, None)
                    S.op("pe", fg, reads=[r_ws[s]] + r_h, writes=[r_ps[bg]])
                    bu, fu = proj_fm(lambda k, wv=wv: wv[:, k, 128:256], lambda k: hT[:, k, HALO:HALO + T], None, KC, T, None)
                    S.op("pe", fu, reads=[r_ws[s]] + r_h, writes=[r_ps[bu]])
                done_piece()
                k = nxt("sg", 2)
                S.op("dve", lambda e, k=k, bg=bg: e.tensor_tensor(out=sgr[:, k, :], in0=psb[bg][:], in1=rstd[:, rk, :],
                                                                  op=ALU.mult),
                     reads=[r_ps[bg], r_rstd[rk]], writes=[r_sg[k]])
                S.op("act", lambda e, k=k: e.activation(out=sgr[:, k, :], in_=sgr[:, k, :], func=AF.Silu),
                     reads=[], writes=[r_sg[k]])
                S.op("dve", lambda e, k=k: e.tensor_tensor(out=sgr[:, k, :], in0=sgr[:, k, :], in1=rstd[:, rk, :],
                                                           op=ALU.mult),
                     reads=[r_rstd[rk]], writes=[r_sg[k]])
                S.op("dve", lambda e, k=k, bu=bu, j=j: e.tensor_tensor(out=hid[:, j, :], in0=psb[bu][:],
                                                                       in1=sgr[:, k, :], op=ALU.mult),
                     reads=[r_ps[bu], r_sg[k]], writes=[r_hid[j]])
            ssb = out_proj(hid, r_hid, HC, None, r_h + r_vn)
            postnorm_residual(gpost, ssb, after_chunk)

        def attn_layer(t):
            first = (t == 0)
            rk0 = prenorm(0, r_Y + r_vn, tokmajor=True)
            S.guard(B_attn, B_ffn + B_gmlp)
            if first:
                S.op("sp", lambda e: e.dma_start(out=xh, in_=xh_in.rearrange("c p n -> p c n")),
                     writes=r_xh, sem="xhld", inc=16)
                ssb = stats_begin()
                for c in range(KC):
                    stats_add(ssb, xh[:, c, :], r_xh, c == 0, c == KC - 1, n=HALO)
                rk = stats_finish(ssb, n=HALO)
                for c in range(KC):
                    S.op("dve", lambda e, c=c: e.scalar_tensor_tensor(
                        out=hT[:, c, 0:HALO], in0=xh[:, c, :], scalar=gcol[:, 0, c:c + 1],
                        in1=rstd[:, rk, 0:HALO], op0=ALU.mult, op1=ALU.mult),
                        reads=r_xh + [r_rstd[rk]], writes=[r_h[c]])
            for c in range(KC):
                if c % 2 == 0:
                    s = next_piece()
                wv = wring[:, s, 0:4096].rearrange("p (k m) -> p k m", k=KC)
                m0 = (c % 2) * 128
                if c == 0:
                    b, b_next = proj_pair_kouter(s, lambda k, wv=wv: wv[:, k, 0:128], lambda k, wv=wv: wv[:, k, 128:256])
                elif c == 1:
                    b = b_next
                else:
                    b, fn = proj_fm(lambda k, wv=wv, m0=m0: wv[:, k, m0:m0 + 128],
                                    lambda k: hT[:, k, HALO:HALO + T], None, KC, T, None)
                    S.op("pe", fn, reads=[r_ws[s]] + r_h, writes=[r_ps[b]])
                if c % 2 == 1:
                    done_piece()
                S.op("dve", lambda e, b=b, c=c: e.scalar_tensor_tensor(
                    out=qT[:, c, :], in0=psb[b][:], scalar=0.125, in1=rstd[:, rk0, :], op0=ALU.mult, op1=ALU.mult),
                    reads=[r_ps[b], r_rstd[rk0]], writes=[r_q[c]])
            for g in range(4):
                if g % 2 == 0:
                    s = next_piece()
                wv = wring[:, s, 0:4096].rearrange("p (k m) -> p k m", k=KC)
                m0 = (g % 2) * 128
                b, fn = proj_fm(lambda k, wv=wv, m0=m0: wv[:, k, m0:m0 + 128],
                                lambda k: hT[:, k, HALO:HALO + T], None, KC, T, None)
                S.op("pe", fn, reads=[r_ws[s]] + r_h, writes=[r_ps[b]])
                S.op("dve", lambda e, b=b, g=g: e.tensor_tensor(out=kdT[:, g, HALO:HALO + T], in0=psb[b][:],
                                                                in1=rstd[:, rk0, :], op=ALU.mult),
                     reads=[r_ps[b], r_rstd[rk0]], writes=[r_kd[g]])
                if first:
                    b2, fn2 = proj_fm(lambda k, wv=wv, m0=m0: wv[:, k, m0:m0 + 128],
                                      lambda k: hT[:, k, 0:HALO], None, KC, HALO, None)
                    S.op("pe", fn2, reads=[r_ws[s]] + r_h, writes=[r_ps[b2]])
                    S.op("dve", lambda e, b2=b2, g=g: e.tensor_copy(out=kdT[:, g, 0:HALO], in_=psb[b2][:, 0:HALO]),
                         reads=[r_ps[b2]], writes=[r_kd[g]])
                if g % 2 == 1:
                    done_piece()
            s = next_piece()
            wv = wring[:, s, 0:4096].rearrange("p (k m) -> p k m", k=KC)
            for blk in range(0 if first else 1, NB + 1):
                c0 = blk * 128
                b, fn = proj_fm(lambda k, c0=c0: hT[:, k, c0:c0 + 128], lambda k, wv=wv: wv[:, k, :],
                                None, KC, 256, None)
                S.op("pe", fn, reads=[r_ws[s]] + r_h, writes=[r_ps[b]])
                src = psb[b][:, 0:256].rearrange("p (g d) -> p g d", g=4)
                if blk == 0:
                    S.op("act", lambda e, blk=blk, src=src: e.copy(out=VV[:, blk, :, 0:64], in_=src),
                         reads=[r_ps[b]], writes=[r_vv[blk]])
                    S.op("dve", lambda e, blk=blk, src=src: e.tensor_copy(out=VV[:, blk, :, 64:128], in_=src),
                         reads=[r_ps[b]], writes=[r_vv[blk]])
                else:
                    S.op("act", lambda e, blk=blk, src=src: e.mul(out=VV[:, blk, :, 0:64], in_=src,
                                                                  mul=rtok[:, blk - 1:blk]),
                         reads=[r_ps[b], r_rtok], writes=[r_vv[blk]])
                    S.op("dve", lambda e, blk=blk, src=src: e.tensor_scalar(out=VV[:, blk, :, 64:128], in0=src,
                                                                            scalar1=rtok[:, blk - 1:blk], scalar2=None,
                                                                            op0=ALU.mult),
                         reads=[r_ps[b], r_rtok], writes=[r_vv[blk]])
            done_piece()

            combos = [(blk, g, half) for blk in range(NB) for g in range(4) for half in range(2)]
            pend = None

            def scores(blk, g, half):
                p0 = half * 64
                outs = []
                for kb in range(2):
                    kc0 = (blk + kb) * 128
                    b = nxt("ps", 6)
                    def fsc(e, b=b, kc0=kc0, kb=kb):
                        e.matmul(psb[b][:].rearrange("p (i q) -> p i q", i=4), lhsT=kdT[p0:p0 + 64, g, kc0:kc0 + 128],
                                 rhs=qT[p0:p0 + 64, 4 * g:4 * g + 4, blk * 128:(blk + 1) * 128], start=True, stop=False)
                        return e.matmul(psb[b][:], lhsT=ident[:], rhs=maskT[:, kb, :], start=False, stop=True)
                    S.op("pe", fsc, reads=[r_kd[g]] + r_q[4 * g:4 * g + 4], writes=[r_ps[b]])
                    tk = nxt("tmp", 4)
                    S.op("dve", lambda e, b=b, tk=tk: e.tensor_tensor(
                        out=tmpr[:, tk, :], in0=psb[b][:],
                        in1=biasC[:, 2 * g + half, :, :].rearrange("p i q -> p (i q)"), op=ALU.add),
                        reads=[r_ps[b]], writes=[r_tmp[tk]])
                    pk = nxt("pt", 4)
                    if kb == 0 and first and blk == 0:
                        S.op("act", lambda e, tk=tk, pk=pk: e.activation(out=ptr[:, pk, :], in_=tmpr[:, tk, :],
                                                                         func=AF.Exp, bias=haloneg[:, 0:1]),
                             reads=[r_tmp[tk]], writes=[r_pt[pk]])
                    else:
                        S.op("act", lambda e, tk=tk, pk=pk: e.activation(out=ptr[:, pk, :], in_=tmpr[:, tk, :],
                                                                         func=AF.Exp),
                             reads=[r_tmp[tk]], writes=[r_pt[pk]])
                    outs.append(pk)
                return outs

            def pv_norm(blk, g, half, pks):
                p0 = half * 64
                bo = nxt("ps", 6)

                def fo(e):
                    e.matmul(psb[bo][:], lhsT=VV[:, blk, g, :], rhs=ptr[:, pks[0], :], start=True, stop=False)
                    return e.matmul(psb[bo][:], lhsT=VV[:, blk + 1, g, :], rhs=ptr[:, pks[1], :], start=False, stop=True)
                S.op("pe", fo, reads=[r_vv[blk], r_vv[blk + 1], r_pt[pks[0]], r_pt[pks[1]]], writes=[r_ps[bo]])
                bd = nxt("ps", 6)

                hsl = slice(8 * g + 4 * half, 8 * g + 4 * half + 4)

                def fd(e):
                    e.matmul(psb[bd][:], lhsT=ones_bf[:], rhs=ptr[:, pks[0], :], start=True, stop=False)
                    e.matmul(psb[bd][:], lhsT=ones_bf[:], rhs=ptr[:, pks[1], :], start=False, stop=False)
                    return e.matmul(psb[bd][:].rearrange("p (i q) -> p i q", i=4), lhsT=ones_bf[0:2, :],
                                    rhs=sinkhl[0:2, hsl].unsqueeze(2).broadcast_to([2, 4, 128]), start=False, stop=True)
                S.op("pe", fd, reads=[r_pt[pks[0]], r_pt[pks[1]]], writes=[r_ps[bd]])
                dk = nxt("den", 2)
                S.op("act", lambda e: e.activation(out=denr[p0:p0 + 64, dk, :], in_=psb[bd][p0:p0 + 64, :], func=AF.Ln),
                     reads=[r_ps[bd]], writes=[r_den[dk]])
                S.op("act", lambda e: e.activation(out=denr[p0:p0 + 64, dk, :], in_=denr[p0:p0 + 64, dk, :], func=AF.Exp,
                                                   scale=-1.0),
                     reads=[], writes=[r_den[dk]])
                S.op("dve", lambda e: e.tensor_tensor(
                    out=qT[p0:p0 + 64, 4 * g:4 * g + 4, blk * 128:(blk + 1) * 128],
                    in0=psb[bo][p0:p0 + 64, :].rearrange("p (i q) -> p i q", i=4),
                    in1=denr[p0:p0 + 64, dk, :].rearrange("p (i q) -> p i q", i=4), op=ALU.mult),
                    reads=[r_ps[bo], r_den[dk]], writes=r_q[4 * g:4 * g + 4])

            for (blk, g, half) in combos:
                pks = scores(blk, g, half)
                if pend is not None:
                    pv_norm(*pend)
                pend = (blk, g, half, pks)
            pv_norm(*pend)
            if t + 1 < n_tiles:
                for g in range(4):
                    S.op("pool", lambda e, g=g: e.tensor_copy(out=kdT[:, g, 0:HALO], in_=kdT[:, g, T:T + HALO]),
                         reads=[], writes=[r_kd[g]])
                S.op("pool", lambda e: e.tensor_copy(out=VV[:, 0, :, :], in_=VV[:, NB, :, :]),
                     reads=[r_vv[NB]], writes=[r_vv[0]])
            ssb = out_proj(qT, r_q, KC, None, r_h + r_vn)
            postnorm_residual(1, ssb)

        def gmlp_layer(t):
            rk4 = prenorm(4, r_Y + r_vn, tokmajor=True)
            S.guard(B_gmlp, B_ffn + B_attn)
            S.guard(r_utmp + r_tmpg + [r_vn0], r_Y)
            for sl in range(8):
                s = next_piece()
                wv = wring[:, s, 0:4096].rearrange("p (k m) -> p k m", k=KC)
                for blk in range(NB):
                    c0 = HALO + blk * 128
                    b, fn = proj_fm(lambda k, c0=c0: hT[:, k, c0:c0 + 128], lambda k, wv=wv: wv[:, k, :],
                                    None, KC, 256, None)
                    S.op("pe", fn, reads=[r_ws[s]] + r_h, writes=[r_ps[b]])
                    S.op("act", lambda e, b=b, blk=blk, sl=sl: e.activation(
                        out=vtok[:, blk, sl * 256:(sl + 1) * 256], in_=psb[b][:, 0:256], func=AF.Gelu_apprx_tanh,
                        scale=rtok[:, blk:blk + 1]),
                        reads=[r_ps[b], r_rtok], writes=[r_vtok[blk]])
                    S.op("dve", lambda e, blk=blk, sl=sl: e.bn_stats(out=lnst[:, blk, sl, :],
                                                                     in_=vtok[:, blk, sl * 256:(sl + 1) * 256]),
                         reads=[r_vtok[blk]], writes=[r_ln[blk]])
                done_piece()
            for blk in range(NB):
                S.op("dve", lambda e, blk=blk: e.bn_aggr(out=lnmv[:, blk, 0:2],
                                                         in_=lnst[:, blk, :, :].rearrange("p a b -> p (a b)")),
                     reads=[], writes=[r_ln[blk]])
                S.op("act", lambda e, blk=blk: e.activation(out=lnmv[:, blk, 2:3], in_=lnmv[:, blk, 1:2], func=AF.Ln,
                                                            bias=epsc[:, 1:2], scale=1.0),
                     reads=[], writes=[r_ln[blk]])
                S.op("act", lambda e, blk=blk: e.activation(out=lnmv[:, blk, 2:3], in_=lnmv[:, blk, 2:3], func=AF.Exp,
                                                            scale=-0.5),
                     reads=[], writes=[r_ln[blk]])
                S.op("dve", lambda e, blk=blk: e.scalar_tensor_tensor(
                    out=lnmv[:, blk, 3:4], in0=lnmv[:, blk, 0:1], scalar=-1.0, in1=lnmv[:, blk, 2:3],
                    op0=ALU.mult, op1=ALU.mult), reads=[], writes=[r_ln[blk]])
            vn_dst = [[r_vn0], [r_vtok[0]], [r_vtok[0]], [r_vtok[1]]]
            for blk in range(NB):
                S.op("dve", lambda e, blk=blk: e.tensor_scalar(
                    out=vnb(blk), in0=vtok[:, blk, :], scalar1=lnmv[:, blk, 2:3], scalar2=lnmv[:, blk, 3:4],
                    op0=ALU.mult, op1=ALU.add), reads=[r_ln[blk], r_vtok[blk]], writes=vn_dst[blk])

            def spatial_gate(g):
                b = nxt("ps", 6)

                def fs(e):
                    ins = None
                    for blk in range(NB):
                        ins = e.matmul(psb[b][:, blk * 128:(blk + 1) * 128], lhsT=vnb(blk)[:, g * 128:(g + 1) * 128],
                                       rhs=WsT[:, g, :], start=True, stop=True)
                    return ins
                S.op("pe", fs, reads=[r_vn0, r_vtok[0], r_vtok[1]], writes=[r_ps[b]])
                tk = nxt("tmpg", 2)
                S.op("dve", lambda e: e.scalar_tensor_tensor(
                    out=tmpg[:, tk, :].rearrange("p (b t) -> p b t", b=NB),
                    in0=psb[b][:].rearrange("p (b t) -> p b t", b=NB), scalar=lncol[:, 0, g:g + 1],
                    in1=Amat[:, g, :].unsqueeze(1).broadcast_to([128, NB, 128]), op0=ALU.mult, op1=ALU.add),
                    reads=[r_ps[b]], writes=[r_tmpg[tk]])
                S.op("dve", lambda e: e.tensor_tensor(out=uT[:, g, :], in0=uT[:, g, :], in1=tmpg[:, tk, :], op=ALU.mult),
                     reads=[r_tmpg[tk]], writes=[r_q[g]])

            for c in range(KC):
                if c % 2 == 0:
                    s = next_piece()
                wv = wring[:, s, 0:4096].rearrange("p (k m) -> p k m", k=KC)
                m0 = (c % 2) * 128
                b, fn = proj_fm(lambda k, wv=wv, m0=m0: wv[:, k, m0:m0 + 128],
                                lambda k: hT[:, k, HALO:HALO + T], None, KC, T, None)
                S.op("pe", fn, reads=[r_ws[s]] + r_h, writes=[r_ps[b]])
                if c % 2 == 1:
                    done_piece()
                uk = nxt("utmp", 2)
                S.op("dve", lambda e, b=b, uk=uk: e.tensor_tensor(out=utmp[:, uk, :], in0=psb[b][:], in1=rstd[:, rk4, :],
                                                                  op=ALU.mult),
                     reads=[r_ps[b], r_rstd[rk4]], writes=[r_utmp[uk]])
                S.op("act", lambda e, uk=uk, c=c: e.activation(out=uT[:, c, :], in_=utmp[:, uk, :], func=AF.Gelu_apprx_tanh),
                     reads=[r_utmp[uk]], writes=[r_q[c]])
                if c >= 2:
                    spatial_gate(c - 2)
            spatial_gate(KC - 2)
            spatial_gate(KC - 1)
            ssb = out_proj(uT, r_q, KC, None, r_h + r_vn + r_utmp + r_tmpg + [r_vn0])
            postnorm_residual(5, ssb)

        def load_x_chunk(t, c):
            S.op("sp" if t == 0 else "pool", lambda e: e.dma_start(out=xT[:, c, :], in_=x_in[c, :, t * T:(t + 1) * T]),
                 writes=[r_x[c]], sem=("xld%d" if t == 0 else "xlp%d") % c, inc=16)

        last_layer = max(layers)
        for t in range(n_tiles):
            if t == 0:
                for c in range(KC):
                    load_x_chunk(0, c)

            def after_chunk(c, t=t):
                S.op("sp", lambda e: e.dma_start(out=out_d[c, :, t * T:(t + 1) * T], in_=xT[:, c, :]),
                     reads=[r_x[c]], sem="ost%d" % c, inc=16)
                if t + 1 < n_tiles:
                    load_x_chunk(t + 1, c)
            if 0 in layers:
                attn_layer(t)
                ffn(0, 2, 3, after_chunk if last_layer == 0 else None)
            if 1 in layers:
                gmlp_layer(t)
                ffn(1, 6, 7, after_chunk)
        S.final_wait("sp", ["ost%d" % c for c in range(KC)])

        with nc.Block() as block:
            @block.sync
            def _(e):
                S.replay("sp", e)

            @block.gpsimd
            def _(e):
                S.replay("pool", e)

            @block.scalar
            def _(e):
                S.replay("act", e)

            @block.vector
            def _(e):
                S.replay("dve", e)

            @block.tensor
            def _(e):
                S.replay("pe", e)
    return nc


def _t5_bucket(dist):
    max_exact = 16
    d_f = np.maximum(dist, 1).astype(np.float32)
    large = max_exact + (np.log(d_f / np.float32(max_exact)) / np.float32(math.log(128 / max_exact))
                         * np.float32(32 - max_exact)).astype(np.int32)
    large = np.minimum(large, 31)
    return np.where(dist < max_exact, dist, large)


def _colpieces(W):
    K, N = W.shape
    n = N // 256
    return np.ascontiguousarray(W.reshape(KC, 128, n, 256).transpose(2, 1, 0, 3)).reshape(n, 128, KC * 256)


def _prep_shared(inp):
    f = np.float32
    wqkv = inp["attn_w_qkv"][0]
    wo = inp["attn_w_o"][0]
    q_p = _colpieces(wqkv[:, 0:2048])
    kd = np.concatenate([np.concatenate([wqkv[:, 2048 + g * 64:2048 + (g + 1) * 64]] * 2, axis=1) for g in range(4)], axis=1)
    kd_p = _colpieces(kd)
    v_p = _colpieces(wqkv[:, 2304:2560])
    wo_p = _colpieces(wo)

    def gu_pieces(wgu):
        gate = wgu[:, :HC * 128].reshape(D, HC, 128)
        up = wgu[:, HC * 128:].reshape(D, HC, 128)
        inter = np.concatenate([gate, up], axis=2).reshape(D, HC * 256)
        return _colpieces(inter)

    def dn_pieces(wd):
        return np.ascontiguousarray(wd.reshape(HC, 128, KC, 128).transpose(2, 1, 0, 3)).reshape(KC, 128, HC * 128)

    wA0 = np.concatenate([q_p, kd_p, v_p, wo_p, gu_pieces(inp["ffn_w_gate_up"][0])], axis=0)
    wD0 = dn_pieces(inp["ffn_w_down"][0])
    win = inp["gmlp_w_in"][0]
    wA1 = np.concatenate([_colpieces(win[:, 2048:4096]), _colpieces(win[:, 0:2048]),
                          _colpieces(inp["gmlp_w_out"][0]), gu_pieces(inp["ffn_w_gate_up"][1])], axis=0)
    wD1 = dn_pieces(inp["ffn_w_down"][1])
    assert wA0.shape[0] == NA[0] and wA1.shape[0] == NA[1]

    ng = inp["norm_gains"].reshape(8, KC, 128)
    gcol = np.ascontiguousarray(ng.transpose(2, 0, 1)).reshape(128, 8 * KC)
    j = np.arange(128)[:, None]
    q = np.arange(128)[None, :]
    dist = np.where(j <= q, q - j, q + 128 - j).astype(np.int32)
    bucket = _t5_bucket(dist)
    tab = inp["rel_bias_table"]
    hb = tab[bucket]
    hb = hb.reshape(128, 128, 4, 4, 2)
    biasC = np.ascontiguousarray(hb.transpose(0, 2, 4, 3, 1)).reshape(128, 32 * 128)
    sk = inp["attn_sinks"][0].reshape(4, 4, 2).transpose(0, 2, 1).reshape(32)
    sinkb = np.ascontiguousarray(np.broadcast_to(sk[None, :], (128, 32))).astype(f)
    lncol = np.stack([inp["gmlp_ln_gain"][0].reshape(KC, 128).T, inp["gmlp_ln_bias"][0].reshape(KC, 128).T], axis=1)
    lncol = np.ascontiguousarray(lncol).reshape(128, 2 * KC)
    bspb = np.ascontiguousarray(np.broadcast_to(inp["gmlp_b_spatial"][0].reshape(1, KC * 128), (128, KC * 128)))
    wst = np.ascontiguousarray(inp["gmlp_w_spatial"][0].transpose(2, 0, 1)).reshape(128, KC * 128)
    return {"wA0": wA0, "wD0": wD0, "wA1": wA1, "wD1": wD1, "gcol": gcol.astype(f), "biasC": biasC.astype(f),
            "sinkb": sinkb, "lncol": lncol.astype(f), "bspb": bspb.astype(f), "wst": wst.astype(f)}


def run_module(inputs, layers=(0, 1), trace=False):
    x = np.asarray(inputs["x"], dtype=np.float32)
    B, Sq, _ = x.shape
    cps = N_CORES // B
    tpc = Sq // cps
    n_tiles = tpc // T
    shared = _prep_shared({k: np.asarray(v, dtype=np.float32) for k, v in inputs.items() if k != "x"})
    in_maps = []
    for core in range(N_CORES):
        b, part = divmod(core, cps)
        s0 = part * tpc
        m = dict(shared)
        m["x_in"] = np.ascontiguousarray(x[b, s0:s0 + tpc, :].T).reshape(KC, 128, tpc)
        if part == 0:
            m["xh_in"] = np.zeros((KC, 128, HALO), np.float32)
            m["haloneg"] = np.full((128, 1), NEG, np.float32)
        else:
            m["xh_in"] = np.ascontiguousarray(x[b, s0 - HALO:s0, :].T).reshape(KC, 128, HALO)
            m["haloneg"] = np.zeros((128, 1), np.float32)
        in_maps.append(m)
    nc = build_program(n_tiles, layers)
    res = run_bass_kernel_spmd(nc, in_maps, core_ids=list(range(N_CORES)), trace=trace)
    out = np.empty((B, Sq, D), np.float32)
    for core in range(N_CORES):
        b, part = divmod(core, cps)
        s0 = part * tpc
        out[b, s0:s0 + tpc, :] = np.asarray(res.results[core]["out"]).reshape(D, tpc).T
    return out, res


def kernel(**inputs):
    out, _ = run_module(inputs)
    return out
```

```python
import contextlib
import math

import numpy as np

import concourse.bass as bass
import concourse.mybir as mybir
from concourse.bass_utils import run_bass_kernel_spmd

F32 = mybir.dt.float32
BF16 = mybir.dt.bfloat16
AF = mybir.ActivationFunctionType
ALU = mybir.AluOpType

N_CORES = 8
D = 2048
KC = 16
T = 512
NB = 4
HC = 44
HALO = 128
NSLOT = 4
SLOT_E = 5632
NORM_EPS = 1e-6
LN_EPS = 1e-5
NEG = -1e30
SAME_ENGINE_SYNC = True
NCONV = 12
CONV_LOOK = 20

A0_Q, A0_KD, A0_V, A0_WO, A0_GU = 0, 8, 10, 11, 19
A1_U, A1_V, A1_WOUT, A1_GU = 0, 8, 16, 24
NA = (63, 68)


class Res:
    __slots__ = ("name", "w", "r", "guard")

    def __init__(self, name):
        self.name = name
        self.w = None
        self.r = {}
        self.guard = {}


class Sched:
    def __init__(self, nc, stack):
        self.nc = nc
        self.stack = stack
        self.engs = ["pe", "act", "dve", "pool", "sp"]
        self.sems = {}
        self.cnt = {}
        self.waited = {e: {} for e in self.engs}
        self.streams = {e: [] for e in self.engs}
        for e in self.engs:
            self.new_sem(e)

    def new_sem(self, key):
        self.sems[key] = self.stack.enter_context(self.nc.semaphore("s_" + key))
        self.cnt[key] = 0

    def op(self, eng, fn, reads=(), writes=(), sem=None, inc=1):
        deps = {}

        def add(k, v):
            if deps.get(k, 0) < v:
                deps[k] = v

        for r in reads:
            if r.w is not None:
                add(*r.w)
        for w in writes:
            if w.w is not None:
                add(*w.w)
            for k, v in w.r.items():
                add(k, v)
            for k, v in w.guard.items():
                add(k, v)
            w.guard = {}
        if eng == "pe" or not SAME_ENGINE_SYNC:
            deps.pop(eng, None)
        st = self.streams[eng]
        wd = self.waited[eng]
        for k, v in deps.items():
            if wd.get(k, 0) < v:
                wd[k] = v
                st.append(("wait", k, v))
        key = sem or eng
        self.cnt[key] += inc
        val = self.cnt[key]
        st.append(("op", fn, key, inc))
        for r in reads:
            if r.r.get(key, 0) < val:
                r.r[key] = val
        for w in writes:
            w.w = (key, val)
            w.r = {}
        return (key, val)

    def snapshot(self, res_list):
        deps = {}
        for r in res_list:
            if r.w is not None and deps.get(r.w[0], 0) < r.w[1]:
                deps[r.w[0]] = r.w[1]
            for k, v in r.r.items():
                if deps.get(k, 0) < v:
                    deps[k] = v
        return deps

    def guard(self, new_list, old_list):
        deps = self.snapshot(old_list)
        for r in new_list:
            for k, v in deps.items():
                if r.guard.get(k, 0) < v:
                    r.guard[k] = v

    def barrier(self, keys):
        for e in self.engs:
            for k in keys:
                v = self.cnt[k]
                if v > 0 and self.waited[e].get(k, 0) < v and k != e:
                    self.waited[e][k] = v
                    self.streams[e].append(("wait", k, v))

    def final_wait(self, eng, keys):
        for k in keys:
            self.streams[eng].append(("wait", k, self.cnt[k]))

    def replay(self, eng, e):
        for item in self.streams[eng]:
            if item[0] == "wait":
                e.wait_ge(self.sems[item[1]], item[2])
            else:
                _, fn, key, inc = item
                ins = fn(e)
                ins.then_inc(self.sems[key], inc)


def build_program(n_tiles, layers=(0, 1)):
    nc = bass.Bass("TRN2", target_bir_lowering=False)
    tpc = n_tiles * T

    def din(name, shape, dt=F32):
        return nc.dram_tensor(name, list(shape), dt, kind="ExternalInput").ap()

    x_in = din("x_in", [KC, 128, tpc])
    xh_in = din("xh_in", [KC, 128, HALO])
    out_d = nc.dram_tensor("out", [KC, 128, tpc], F32, kind="ExternalOutput").ap()
    wA = [din("wA0", [NA[0], 128, 4096]), din("wA1", [NA[1], 128, 4096])]
    wD = [din("wD0", [16, 128, SLOT_E]), din("wD1", [16, 128, SLOT_E])]
    wAb = [nc.dram_tensor("wA%db" % l, [NA[l], 128, 4096], BF16, kind="Internal").ap() for l in range(2)]
    wDb = [nc.dram_tensor("wD%db" % l, [16, 128, SLOT_E], BF16, kind="Internal").ap() for l in range(2)]
    gcol_d = din("gcol", [128, 8 * KC])
    biasC_d = din("biasC", [128, 32 * 128])
    sinkb_d = din("sinkb", [128, 32])
    haloneg_d = din("haloneg", [128, 1])
    lncol_d = din("lncol", [128, 2 * KC])
    bsp_d = din("bspb", [128, KC * 128])
    wst_d = din("wst", [128, KC * 128])

    with contextlib.ExitStack() as stack:
        S = Sched(nc, stack)

        def sb(name, shape, dt):
            return stack.enter_context(nc.sbuf_tensor("sb_" + name, list(shape), dt))

        def ps(name):
            return stack.enter_context(nc.psum_tensor(name, [128, 512], F32))

        xT = sb("xT", [128, KC, T], F32)
        AY = sb("AY", [128, KC * T], F32)
        Bar = sb("Bar", [128, 12288], F32)
        kdT = sb("kdT", [128, 4, HALO + T], BF16)
        VV = sb("VV", [128, NB + 1, 4, 128], BF16)
        biasC = sb("biasC", [128, 8, 4, 128], F32)
        Amat = sb("Amat", [128, KC, 128], F32)
        WsT = sb("WsT", [128, KC, 128], BF16)
        gcol = sb("gcol", [128, 8, KC], F32)
        lncol = sb("lncol", [128, 2, KC], F32)
        sinkb = sb("sinkb", [128, 32], F32)
        expsink = sb("expsink", [128, 32], F32)
        haloneg = sb("haloneg", [128, 1], F32)
        mmat = sb("mmat", [128, 128], BF16)
        ones_bf = sb("ones_bf", [128, 128], BF16)
        ones_f = sb("ones_f", [128, 128], F32)
        wring = sb("wring", [128, NSLOT, SLOT_E], BF16)
        sqring = sb("sqring", [128, 3, T], BF16)
        rstd = sb("rstd", [128, 2, T], F32)
        lnst = sb("lnst", [128, NB, 8, 6], F32)
        lnmv = sb("lnmv", [128, NB, 4], F32)
        ident = sb("ident", [128, 128], BF16)
        maskT = sb("maskT", [128, 2, T], BF16)
        sinkhl = sb("sinkhl", [128, 32], BF16)
        sinktmp = sb("sinktmp", [128, 2, 32], F32)
        m01 = sb("m01", [128, 2], F32)
        epsc = sb("epsc", [128, 2], F32)
        inv128 = sb("inv128", [128, 1], F32)
        rtok = sb("rtok", [128, NB], F32)
        print("sbuf bytes remaining", nc.sbuf_bytes_remaining)

        AYb = AY[:].bitcast(BF16)
        hT = AYb[:, 0:KC * (HALO + T)].rearrange("p (c n) -> p c n", c=KC)
        Y = AY[:].rearrange("p (c n) -> p c n", c=KC)
        vn = AYb[:, 0:NB * D].rearrange("p (b f) -> p b f", b=NB)
        wsf = AY[:, 0:KC * 128].rearrange("p (g t) -> p g t", g=KC)
        utmp = AY[:, 6144:6144 + 2 * T].rearrange("p (k n) -> p k n", k=2)
        tmpg = AY[:, 5120:5120 + 2 * T].rearrange("p (k n) -> p k n", k=2)
        Bb = Bar[:].bitcast(BF16)
        qT = Bb[:, 0:KC * T].rearrange("p (c n) -> p c n", c=KC)
        tmpr = Bar[:, 4096:4096 + 4 * T].rearrange("p (k n) -> p k n", k=4)
        ptr = Bb[:, 12288:12288 + 4 * T].rearrange("p (k n) -> p k n", k=4)
        denr = Bar[:, 7168:7168 + 2 * T].rearrange("p (k n) -> p k n", k=2)
        xh = Bar[:, 10240:10240 + KC * HALO].rearrange("p (c n) -> p c n", c=KC)
        uT = qT
        vtok = Bar[:, 4096:4096 + NB * D].rearrange("p (b f) -> p b f", b=NB)
        hid = Bb[:, 0:HC * T].rearrange("p (j n) -> p j n", j=HC)

        def vnb(blk):
            if blk == 0:
                return AYb[:, 14336:14336 + D]
            o = 8192 + (blk - 1) * D
            return Bb[:, o:o + D]
        sgr = Bar[:, 11264:11264 + 2 * T].rearrange("p (k n) -> p k n", k=2)

        psb = [ps("ps%d" % i) for i in range(8)]

        def RL(name, n):
            return [Res("%s%d" % (name, i)) for i in range(n)]

        r_x = RL("x", KC)
        r_h = RL("h", KC)
        r_Y = RL("Y", KC)
        r_vn = []
        r_vn0 = Res("vn0")
        r_q = RL("q", KC)
        r_tmp = RL("tmp", 4)
        r_pt = RL("pt", 4)
        r_den = RL("den", 2)
        r_xh = RL("xh", 1)
        r_vtok = RL("vtok", NB)
        r_hid = RL("hid", HC)
        r_sg = RL("sg", 2)
        r_kd = RL("kd", 4)
        r_vv = RL("vv", NB + 1)
        r_ps = RL("ps", 6)
        r_ss = RL("ss", 2)
        r_sq = RL("sq", 3)
        r_rstd = RL("rstd", 2)
        r_ws = RL("ws", NSLOT)
        r_ln = RL("ln", NB)
        r_setup = Res("setup")
        r_tmpg = RL("tmpg", 2)
        r_rtok = Res("rtok")
        r_utmp = RL("utmp", 2)
        r_scrA = [RL("scrA0_", NA[0]), RL("scrA1_", NA[1])]
        r_scrD = [RL("scrD0_", 16), RL("scrD1_", 16)]
        r_conv = RL("conv", NCONV)
        for i in range(NSLOT):
            S.new_sem("wslot%d" % i)
        for i in range(NSLOT):
            S.new_sem("wcast%d" % i)
            S.new_sem("wstore%d" % i)
        S.new_sem("setupdma")
        S.new_sem("xhld")
        for c in range(KC):
            S.new_sem("xld%d" % c)
            S.new_sem("xlp%d" % c)
            S.new_sem("ost%d" % c)

        AY_groups = [r_h, r_Y, r_vn]
        B_attn = r_q + r_tmp + r_pt + r_den + r_xh
        B_gmlp = r_q + r_vtok
        B_ffn = r_hid + r_sg

        state = {"ps": 0, "ss": 0, "sq": 0, "rstd": 0, "tmp": 0, "pt": 0, "den": 0, "sg": 0,
                 "ev": 0, "tmpg": 0, "utmp": 0}

        def nxt(name, n):
            v = state[name]
            state[name] = (v + 1) % n
            return v

        pieces = []
        for t in range(n_tiles):
            for l in range(2):
                if l not in layers:
                    continue
                for i in range(NA[l]):
                    pieces.append((wAb[l][i], 4096, r_scrA[l][i], wA[l][i], t))
                    if i == NA[l] - 1:
                        for c in range(16):
                            pieces.append((wDb[l][c], SLOT_E, r_scrD[l][c], wD[l][c], t))
        wstate = {"next_load": 0, "next_use": 0}
        ppt = len(pieces) // n_tiles

        def issue_load():
            i = wstate["next_load"]
            if i >= len(pieces):
                return
            wstate["next_load"] = i + 1
            scr, ne, res, src32, t = pieces[i]
            s = i % NSLOT
            odd = (i % ppt) % 2 == 1
            cast = (t == 0) or (t == 1 and odd)
            store = ((t == 0 and not odd) or (t == 1 and odd) or (t == 0 and n_tiles == 2)) and t + 1 < n_tiles
            if cast:
                S.op("pool", lambda e: e.dma_start(out=wring[:, s, 0:ne], in_=src32),
                     reads=[], writes=[r_ws[s]], sem="wcast%d" % s, inc=16)
                if store:
                    S.op("sp", lambda e: e.dma_start(out=scr, in_=wring[:, s, 0:ne]),
                         reads=[r_ws[s]], writes=[res], sem="wstore%d" % s, inc=16)
            else:
                S.op("sp", lambda e: e.dma_start(out=wring[:, s, 0:ne], in_=scr),
                     reads=[res], writes=[r_ws[s]], sem="wslot%d" % s, inc=16)

        def next_piece():
            i = wstate["next_use"]
            wstate["next_use"] = i + 1
            return i % NSLOT

        def done_piece():
            issue_load()

        def sdma(dst, src):
            S.op("sp", lambda e: e.dma_start(out=dst, in_=src), writes=[r_setup], sem="setupdma", inc=16)

        sdma(gcol[:].rearrange("p a c -> p (a c)"), gcol_d)
        sdma(biasC[:].rearrange("p a i q -> p (a i q)"), biasC_d)
        sdma(sinkb[:], sinkb_d)
        sdma(haloneg[:], haloneg_d)
        sdma(lncol[:].rearrange("p a c -> p (a c)"), lncol_d)
        sdma(Amat[:].rearrange("p g t -> p (g t)"), bsp_d)
        sdma(AY[:, 0:KC * 128], wst_d)
        r_c = [Res("c%d" % i) for i in range(8)]
        S.op("pool", lambda e: e.memset(mmat[:], 1.0 / D), writes=[r_c[0]])
        S.op("pool", lambda e: e.memset(ones_bf[:], 1.0), writes=[r_c[1]])
        S.op("pool", lambda e: e.memset(ones_f[:], 1.0), writes=[r_c[2]])
        S.op("pool", lambda e: e.memset(epsc[:, 0:1], NORM_EPS), writes=[r_c[7]])
        S.op("pool", lambda e: e.memset(epsc[:, 1:2], LN_EPS), writes=[r_c[7]])
        S.op("pool", lambda e: e.memset(inv128[:], 1.0 / 128), writes=[r_c[7]])
        S.op("pool", lambda e: e.affine_select(out=wsf, in_=wsf, pattern=[[0, KC], [1, 128]],
                                               compare_op=ALU.is_ge, fill=0.0, base=0,
                                               channel_multiplier=-1),
             reads=[r_setup], writes=[r_c[3]])
        S.op("act", lambda e: e.copy(out=WsT[:], in_=wsf), reads=[r_c[3]], writes=[r_c[4]])
        S.op("act", lambda e: e.activation(out=expsink[:], in_=sinkb[:], func=AF.Exp),
             reads=[r_setup], writes=[r_c[5]])
        S.op("pool", lambda e: e.memset(m01[:], 1.0), writes=[r_c[5]])
        S.op("pool", lambda e: e.affine_select(out=m01[:, 0:1], in_=m01[:, 0:1], pattern=[[0, 1]], compare_op=ALU.is_equal,
                                               fill=0.0, base=0, channel_multiplier=1), writes=[r_c[5]])
        S.op("pool", lambda e: e.affine_select(out=m01[:, 1:2], in_=m01[:, 1:2], pattern=[[0, 1]], compare_op=ALU.is_equal,
                                               fill=0.0, base=-1, channel_multiplier=1), writes=[r_c[5]])
        S.op("dve", lambda e: e.tensor_copy(out=sinkhl[:], in_=expsink[:]), reads=[r_c[5]], writes=[r_c[5]])
        S.op("dve", lambda e: e.tensor_copy(out=sinktmp[:, 0, :], in_=sinkhl[:]), reads=[], writes=[r_c[5]])
        S.op("dve", lambda e: e.tensor_tensor(out=sinktmp[:, 1, :], in0=expsink[:], in1=sinktmp[:, 0, :],
                                              op=ALU.subtract), reads=[], writes=[r_c[5]])
        S.op("dve", lambda e: e.tensor_scalar(out=sinktmp[:, 0, :], in0=sinktmp[:, 0, :], scalar1=m01[:, 0:1],
                                              scalar2=None, op0=ALU.mult), reads=[], writes=[r_c[5]])
        S.op("dve", lambda e: e.scalar_tensor_tensor(out=sinktmp[:, 0, :], in0=sinktmp[:, 1, :], scalar=m01[:, 1:2],
                                                     in1=sinktmp[:, 0, :], op0=ALU.mult, op1=ALU.add),
             reads=[], writes=[r_c[5]])
        S.op("dve", lambda e: e.tensor_copy(out=sinkhl[:], in_=sinktmp[:, 0, :]), reads=[], writes=[r_c[5]])
        S.op("pool", lambda e: e.memset(ident[:], 1.0), writes=[r_c[7]])
        S.op("pool", lambda e: e.affine_select(out=ident[:], in_=ident[:], pattern=[[-1, 128]], compare_op=ALU.is_equal,
                                               fill=0.0, base=0, channel_multiplier=1), writes=[r_c[7]])
        S.op("pool", lambda e: e.memset(maskT[:], 0.0), writes=[r_c[7]])
        mp = maskT[:, 0, :].rearrange("p (i q) -> p i q", i=4)
        mc = maskT[:, 1, :].rearrange("p (i q) -> p i q", i=4)
        S.op("pool", lambda e: e.affine_select(out=mp, in_=mp, pattern=[[0, 4], [-1, 128]], compare_op=ALU.is_gt,
                                               fill=NEG, base=0, channel_multiplier=1), writes=[r_c[7]])
        S.op("pool", lambda e: e.affine_select(out=mc, in_=mc, pattern=[[0, 4], [1, 128]], compare_op=ALU.is_ge,
                                               fill=NEG, base=0, channel_multiplier=-1), writes=[r_c[7]])
        for _ in range(NSLOT):
            issue_load()
        for k in range(4):
            S.op("pe", lambda e, k=k: e.matmul(psb[k][:].rearrange("p (a t) -> p a t", a=4), lhsT=ones_f[:], rhs=wsf[:, 4 * k:4 * k + 4, :],
                                               start=True, stop=True),
                 reads=[r_c[2], r_c[3]], writes=[r_ps[k]])
            for gg in range(4):
                g = 4 * k + gg
                S.op("dve", lambda e, k=k, gg=gg, g=g: e.scalar_tensor_tensor(
                    out=Amat[:, g, :], in0=psb[k][:, gg * 128:(gg + 1) * 128], scalar=lncol[:, 1, g:g + 1],
                    in1=Amat[:, g, :], op0=ALU.mult, op1=ALU.add),
                    reads=[r_ps[k], r_setup], writes=[r_c[6]])
        S.barrier(["pe", "act", "dve", "pool", "setupdma"])

        def evac_engine():
            v = state["ev"]
            state["ev"] = v ^ 1
            return "act" if v == 0 else "dve"

        def proj_fm(w_ap_fn, rhs_fn, rhs_res, nk, n, evac, M=128):
            b = nxt("ps", 6)

            def fn(e):
                ins = None
                for k in range(nk):
                    ins = e.matmul(psb[b][0:M, 0:n], lhsT=w_ap_fn(k), rhs=rhs_fn(k),
                                   start=(k == 0), stop=(k == nk - 1))
                return ins
            return b, fn

        def proj_pair_kouter(s, wfn_a, wfn_b):
            ba = nxt("ps", 6)
            bb = nxt("ps", 6)
            for k in range(KC):
                def fn(e, k=k):
                    e.matmul(psb[ba][:], lhsT=wfn_a(k), rhs=hT[:, k, HALO:HALO + T], start=(k == 0), stop=(k == KC - 1))
                    return e.matmul(psb[bb][:], lhsT=wfn_b(k), rhs=hT[:, k, HALO:HALO + T], start=(k == 0),
                                    stop=(k == KC - 1))
                S.op("pe", fn, reads=[r_ws[s], r_h[k]], writes=[r_ps[ba], r_ps[bb]])
            return ba, bb

        def stats_begin():
            return nxt("ss", 2)

        def stats_sq(src_ap, src_res, n=T):
            k = nxt("sq", 3)
            S.op("act", lambda e: e.activation(out=sqring[:, k, 0:n], in_=src_ap, func=AF.Square),
                 reads=src_res, writes=[r_sq[k]])
            return k

        def stats_mm(ssb, k, first, last, n=T):
            S.op("pe", lambda e: e.matmul(psb[6 + ssb][:, 0:n], lhsT=mmat[:], rhs=sqring[:, k, 0:n],
                                          start=first, stop=last),
                 reads=[r_sq[k]], writes=[r_ss[ssb]])

        def stats_add(ssb, src_ap, src_res, first, last, n=T):
            k = stats_sq(src_ap, src_res, n)
            stats_mm(ssb, k, first, last, n)

        def stats_finish(ssb, n=T):
            k = nxt("rstd", 2)
            S.op("act", lambda e: e.activation(out=rstd[:, k, 0:n], in_=psb[6 + ssb][:, 0:n], func=AF.Ln,
                                               bias=epsc[:, 0:1], scale=1.0),
                 reads=[r_ss[ssb]], writes=[r_rstd[k]])
            S.op("act", lambda e: e.activation(out=rstd[:, k, 0:n], in_=rstd[:, k, 0:n], func=AF.Exp, scale=-0.5),
                 reads=[], writes=[r_rstd[k]])
            return k

        def prenorm(gidx, guard_old, tokmajor=False):
            others = [r for r in guard_old if r not in r_Y]
            for c in range(KC):
                j0 = (c * 1280) // 2048
                j1 = ((c + 1) * 1280 - 1) // 2048
                ys = r_Y[j0:j1 + 1] if any(r in r_Y for r in guard_old) else []
                S.guard([r_h[c]], ys + others)
            ssb = stats_begin()
            for c in range(KC):
                stats_add(ssb, xT[:, c, :], [r_x[c]], c == 0, c == KC - 1)
                S.op("act", lambda e, c=c: e.mul(out=hT[:, c, HALO:HALO + T], in_=xT[:, c, :],
                                                 mul=gcol[:, gidx, c:c + 1]),
                     reads=[r_x[c]], writes=[r_h[c]])
            rk = stats_finish(ssb)
            if tokmajor:
                b = nxt("ps", 6)

                def ft(e):
                    ins = None
                    for blk in range(NB):
                        ins = e.matmul(psb[b][:, blk:blk + 1], lhsT=rstd[:, rk, blk * 128:(blk + 1) * 128],
                                       rhs=inv128[:, 0:1], start=True, stop=True)
                    return ins
                S.op("pe", ft, reads=[r_rstd[rk]], writes=[r_ps[b]])
                S.op("act", lambda e: e.copy(out=rtok[:, :], in_=psb[b][:, 0:NB]), reads=[r_ps[b]], writes=[r_rtok])
            return rk

        def y_evac(b, c):
            S.op("act", lambda e: e.copy(out=Y[:, c, :], in_=psb[b][:]), reads=[r_ps[b]], writes=[r_Y[c]])
            return stats_sq(psb[b][:], [r_ps[b]])

        def postnorm_residual(gidx, ssb, after_chunk=None):
            rk = stats_finish(ssb)
            for c in range(KC):
                S.op("dve", lambda e, c=c: e.scalar_tensor_tensor(
                    out=Y[:, c, :], in0=Y[:, c, :], scalar=gcol[:, gidx, c:c + 1],
                    in1=rstd[:, rk, :], op0=ALU.mult, op1=ALU.mult),
                    reads=[r_rstd[rk]], writes=[r_Y[c]])
                S.op("dve", lambda e, c=c: e.tensor_tensor(out=xT[:, c, :], in0=xT[:, c, :], in1=Y[:, c, :],
                                                           op=ALU.add),
                     reads=[r_Y[c]], writes=[r_x[c]])
                if after_chunk is not None:
                    after_chunk(c)

        def out_proj(rhs_ap, rhs_res, nk, piece_w, guard_old):
            S.guard(r_Y, guard_old)
            ssb = stats_begin()
            pend = None
            for c in range(KC):
                if nk == KC:
                    if c % 2 == 0:
                        s = next_piece()
                    wv = wring[:, s, 0:4096].rearrange("p (k m) -> p k m", k=KC)
                    m0 = (c % 2) * 128
                    wfn = lambda k, wv=wv, m0=m0: wv[:, k, m0:m0 + 128]
                else:
                    s = next_piece()
                    wv = wring[:, s, 0:SLOT_E].rearrange("p (k m) -> p k m", k=HC)
                    wfn = lambda k, wv=wv: wv[:, k, :]
                b, fn = proj_fm(wfn, lambda k: rhs_ap[:, k, :], None, nk, T, None)
                S.op("pe", fn, reads=[r_ws[s]] + rhs_res, writes=[r_ps[b]])
                if nk != KC or c % 2 == 1:
                    done_piece()
                if pend is not None:
                    stats_mm(ssb, pend[0], pend[1] == 0, False)
                pend = (y_evac(b, c), c)
            stats_mm(ssb, pend[0], False, True)
            return ssb

        def ffn(l, gpre, gpost, after_chunk=None):
            rk = prenorm(gpre, r_Y + r_vn)
            S.guard(B_ffn, B_attn + B_gmlp)
            for j in range(HC):
                s = next_piece()
                wv = wring[:, s, 0:4096].rearrange("p (k m) -> p k m", k=KC)
                if j == 0:
                    bg, bu = proj_pair_kouter(s, lambda k, wv=wv: wv[:, k, 0:128], lambda k, wv=wv: wv[:, k, 128:256])
                else:
                    bg, fg = proj_fm(lambda k, wv=wv: wv[:, k, 0:128], lambda k: hT[:, k, HALO:HALO + T], None, KC, T, None)
                    S.op("pe", fg, reads=[r_ws[s]] + r_h, writes=[r_ps[bg]])
                    bu, fu = proj_fm(lambda k, wv=wv: wv[:, k, 128:256], lambda k: hT[:, k, HALO:HALO + T], None, KC, T, None)
                    S.op("pe", fu, reads=[r_ws[s]] + r_h, writes=[r_ps[bu]])
                done_piece()
                k = nxt("sg", 2)
                S.op("dve", lambda e, k=k, bg=bg: e.tensor_tensor(out=sgr[:, k, :], in0=psb[bg][:], in1=rstd[:, rk, :],
                                                                  op=ALU.mult),
                     reads=[r_ps[bg], r_rstd[rk]], writes=[r_sg[k]])
                S.op("act", lambda e, k=k: e.activation(out=sgr[:, k, :], in_=sgr[:, k, :], func=AF.Silu),
                     reads=[], writes=[r_sg[k]])
                S.op("dve", lambda e, k=k: e.tensor_tensor(out=sgr[:, k, :], in0=sgr[:, k, :], in1=rstd[:, rk, :],
                                                           op=ALU.mult),
                     reads=[r_rstd[rk]], writes=[r_sg[k]])
                S.op("dve", lambda e, k=k, bu=bu, j=j: e.tensor_tensor(out=hid[:, j, :], in0=psb[bu][:],
                                                                       in1=sgr[:, k, :], op=ALU.mult),
                     reads=[r_ps[bu], r_sg[k]], writes=[r_hid[j]])
            ssb = out_proj(hid, r_hid, HC, None, r_h + r_vn)
            postnorm_residual(gpost, ssb, after_chunk)

        def attn_layer(t):
            first = (t == 0)
            rk0 = prenorm(0, r_Y + r_vn, tokmajor=True)
            S.guard(B_attn, B_ffn + B_gmlp)
            if first:
                S.op("sp", lambda e: e.dma_start(out=xh, in_=xh_in.rearrange("c p n -> p c n")),
                     writes=r_xh, sem="xhld", inc=16)
                ssb = stats_begin()
                for c in range(KC):
                    stats_add(ssb, xh[:, c, :], r_xh, c == 0, c == KC - 1, n=HALO)
                rk = stats_finish(ssb, n=HALO)
                for c in range(KC):
                    S.op("dve", lambda e, c=c: e.scalar_tensor_tensor(
                        out=hT[:, c, 0:HALO], in0=xh[:, c, :], scalar=gcol[:, 0, c:c + 1],
                        in1=rstd[:, rk, 0:HALO], op0=ALU.mult, op1=ALU.mult),
                        reads=r_xh + [r_rstd[rk]], writes=[r_h[c]])
            for c in range(KC):
                if c % 2 == 0:
                    s = next_piece()
                wv = wring[:, s, 0:4096].rearrange("p (k m) -> p k m", k=KC)
                m0 = (c % 2) * 128
                if c == 0:
                    b, b_next = proj_pair_kouter(s, lambda k, wv=wv: wv[:, k, 0:128], lambda k, wv=wv: wv[:, k, 128:256])
                elif c == 1:
                    b = b_next
                else:
                    b, fn = proj_fm(lambda k, wv=wv, m0=m0: wv[:, k, m0:m0 + 128],
                                    lambda k: hT[:, k, HALO:HALO + T], None, KC, T, None)
                    S.op("pe", fn, reads=[r_ws[s]] + r_h, writes=[r_ps[b]])
                if c % 2 == 1:
                    done_piece()
                S.op("dve", lambda e, b=b, c=c: e.scalar_tensor_tensor(
                    out=qT[:, c, :], in0=psb[b][:], scalar=0.125, in1=rstd[:, rk0, :], op0=ALU.mult, op1=ALU.mult),
                    reads=[r_ps[b], r_rstd[rk0]], writes=[r_q[c]])
            for g in range(4):
                if g % 2 == 0:
                    s = next_piece()
                wv = wring[:, s, 0:4096].rearrange("p (k m) -> p k m", k=KC)
                m0 = (g % 2) * 128
                b, fn = proj_fm(lambda k, wv=wv, m0=m0: wv[:, k, m0:m0 + 128],
                                lambda k: hT[:, k, HALO:HALO + T], None, KC, T, None)
                S.op("pe", fn, reads=[r_ws[s]] + r_h, writes=[r_ps[b]])
                S.op("dve", lambda e, b=b, g=g: e.tensor_tensor(out=kdT[:, g, HALO:HALO + T], in0=psb[b][:],
                                                                in1=rstd[:, rk0, :], op=ALU.mult),
                     reads=[r_ps[b], r_rstd[rk0]], writes=[r_kd[g]])
                if first:
                    b2, fn2 = proj_fm(lambda k, wv=wv, m0=m0: wv[:, k, m0:m0 + 128],
                                      lambda k: hT[:, k, 0:HALO], None, KC, HALO, None)
                    S.op("pe", fn2, reads=[r_ws[s]] + r_h, writes=[r_ps[b2]])
                    S.op("dve", lambda e, b2=b2, g=g: e.tensor_copy(out=kdT[:, g, 0:HALO], in_=psb[b2][:, 0:HALO]),
                         reads=[r_ps[b2]], writes=[r_kd[g]])
                if g % 2 == 1:
                    done_piece()
            s = next_piece()
            wv = wring[:, s, 0:4096].rearrange("p (k m) -> p k m", k=KC)
            for blk in range(0 if first else 1, NB + 1):
                c0 = blk * 128
                b, fn = proj_fm(lambda k, c0=c0: hT[:, k, c0:c0 + 128], lambda k, wv=wv: wv[:, k, :],
                                None, KC, 256, None)
                S.op("pe", fn, reads=[r_ws[s]] + r_h, writes=[r_ps[b]])
                src = psb[b][:, 0:256].rearrange("p (g d) -> p g d", g=4)
                if blk == 0:
                    S.op("act", lambda e, blk=blk, src=src: e.copy(out=VV[:, blk, :, 0:64], in_=src),
                         reads=[r_ps[b]], writes=[r_vv[blk]])
                    S.op("dve", lambda e, blk=blk, src=src: e.tensor_copy(out=VV[:, blk, :, 64:128], in_=src),
                         reads=[r_ps[b]], writes=[r_vv[blk]])
                else:
                    S.op("act", lambda e, blk=blk, src=src: e.mul(out=VV[:, blk, :, 0:64], in_=src,
                                                                  mul=rtok[:, blk - 1:blk]),
                         reads=[r_ps[b], r_rtok], writes=[r_vv[blk]])
                    S.op("dve", lambda e, blk=blk, src=src: e.tensor_scalar(out=VV[:, blk, :, 64:128], in0=src,
                                                                            scalar1=rtok[:, blk - 1:blk], scalar2=None,
                                                                            op0=ALU.mult),
                         reads=[r_ps[b], r_rtok], writes=[r_vv[blk]])
            done_piece()

            combos = [(blk, g, half) for blk in range(NB) for g in range(4) for half in range(2)]
            pend = None

            def scores(blk, g, half):
                p0 = half * 64
                outs = []
                for kb in range(2):
                    kc0 = (blk + kb) * 128
                    b = nxt("ps", 6)
                    def fsc(e, b=b, kc0=kc0, kb=kb):
                        e.matmul(psb[b][:].rearrange("p (i q) -> p i q", i=4), lhsT=kdT[p0:p0 + 64, g, kc0:kc0 + 128],
                                 rhs=qT[p0:p0 + 64, 4 * g:4 * g + 4, blk * 128:(blk + 1) * 128], start=True, stop=False)
                        return e.matmul(psb[b][:], lhsT=ident[:], rhs=maskT[:, kb, :], start=False, stop=True)
                    S.op("pe", fsc, reads=[r_kd[g]] + r_q[4 * g:4 * g + 4], writes=[r_ps[b]])
                    tk = nxt("tmp", 4)
                    S.op("dve", lambda e, b=b, tk=tk: e.tensor_tensor(
                        out=tmpr[:, tk, :], in0=psb[b][:],
                        in1=biasC[:, 2 * g + half, :, :].rearrange("p i q -> p (i q)"), op=ALU.add),
                        reads=[r_ps[b]], writes=[r_tmp[tk]])
                    pk = nxt("pt", 4)
                    if kb == 0 and first and blk == 0:
                        S.op("act", lambda e, tk=tk, pk=pk: e.activation(out=ptr[:, pk, :], in_=tmpr[:, tk, :],
                                                                         func=AF.Exp, bias=haloneg[:, 0:1]),
                             reads=[r_tmp[tk]], writes=[r_pt[pk]])
                    else:
                        S.op("act", lambda e, tk=tk, pk=pk: e.activation(out=ptr[:, pk, :], in_=tmpr[:, tk, :],
                                                                         func=AF.Exp),
                             reads=[r_tmp[tk]], writes=[r_pt[pk]])
                    outs.append(pk)
                return outs

            def pv_norm(blk, g, half, pks):
                p0 = half * 64
                bo = nxt("ps", 6)

                def fo(e):
                    e.matmul(psb[bo][:], lhsT=VV[:, blk, g, :], rhs=ptr[:, pks[0], :], start=True, stop=False)
                    return e.matmul(psb[bo][:], lhsT=VV[:, blk + 1, g, :], rhs=ptr[:, pks[1], :], start=False, stop=True)
                S.op("pe", fo, reads=[r_vv[blk], r_vv[blk + 1], r_pt[pks[0]], r_pt[pks[1]]], writes=[r_ps[bo]])
                bd = nxt("ps", 6)

                hsl = slice(8 * g + 4 * half, 8 * g + 4 * half + 4)

                def fd(e):
                    e.matmul(psb[bd][:], lhsT=ones_bf[:], rhs=ptr[:, pks[0], :], start=True, stop=False)
                    e.matmul(psb[bd][:], lhsT=ones_bf[:], rhs=ptr[:, pks[1], :], start=False, stop=False)
                    return e.matmul(psb[bd][:].rearrange("p (i q) -> p i q", i=4), lhsT=ones_bf[0:2, :],
                                    rhs=sinkhl[0:2, hsl].unsqueeze(2).broadcast_to([2, 4, 128]), start=False, stop=True)
                S.op("pe", fd, reads=[r_pt[pks[0]], r_pt[pks[1]]], writes=[r_ps[bd]])
                dk = nxt("den", 2)
                S.op("act", lambda e: e.activation(out=denr[p0:p0 + 64, dk, :], in_=psb[bd][p0:p0 + 64, :], func=AF.Ln),
                     reads=[r_ps[bd]], writes=[r_den[dk]])
                S.op("act", lambda e: e.activation(out=denr[p0:p0 + 64, dk, :], in_=denr[p0:p0 + 64, dk, :], func=AF.Exp,
                                                   scale=-1.0),
                     reads=[], writes=[r_den[dk]])
                S.op("dve", lambda e: e.tensor_tensor(
                    out=qT[p0:p0 + 64, 4 * g:4 * g + 4, blk * 128:(blk + 1) * 128],
                    in0=psb[bo][p0:p0 + 64, :].rearrange("p (i q) -> p i q", i=4),
                    in1=denr[p0:p0 + 64, dk, :].rearrange("p (i q) -> p i q", i=4), op=ALU.mult),
                    reads=[r_ps[bo], r_den[dk]], writes=r_q[4 * g:4 * g + 4])

            for (blk, g, half) in combos:
                pks = scores(blk, g, half)
                if pend is not None:
                    pv_norm(*pend)
                pend = (blk, g, half, pks)
            pv_norm(*pend)
            if t + 1 < n_tiles:
                for g in range(4):
                    S.op("pool", lambda e, g=g: e.tensor_copy(out=kdT[:, g, 0:HALO], in_=kdT[:, g, T:T + HALO]),
                         reads=[], writes=[r_kd[g]])
                S.op("pool", lambda e: e.tensor_copy(out=VV[:, 0, :, :], in_=VV[:, NB, :, :]),
                     reads=[r_vv[NB]], writes=[r_vv[0]])
            ssb = out_proj(qT, r_q, KC, None, r_h + r_vn)
            postnorm_residual(1, ssb)

        def gmlp_layer(t):
            rk4 = prenorm(4, r_Y + r_vn, tokmajor=True)
            S.guard(B_gmlp, B_ffn + B_attn)
            S.guard(r_utmp + r_tmpg + [r_vn0], r_Y)
            for sl in range(8):
                s = next_piece()
                wv = wring[:, s, 0:4096].rearrange("p (k m) -> p k m", k=KC)
                for blk in range(NB):
                    c0 = HALO + blk * 128
                    b, fn = proj_fm(lambda k, c0=c0: hT[:, k, c0:c0 + 128], lambda k, wv=wv: wv[:, k, :],
                                    None, KC, 256, None)
                    S.op("pe", fn, reads=[r_ws[s]] + r_h, writes=[r_ps[b]])
                    S.op("act", lambda e, b=b, blk=blk, sl=sl: e.activation(
                        out=vtok[:, blk, sl * 256:(sl + 1) * 256], in_=psb[b][:, 0:256], func=AF.Gelu_apprx_tanh,
                        scale=rtok[:, blk:blk + 1]),
                        reads=[r_ps[b], r_rtok], writes=[r_vtok[blk]])
                    S.op("dve", lambda e, blk=blk, sl=sl: e.bn_stats(out=lnst[:, blk, sl, :],
                                                                     in_=vtok[:, blk, sl * 256:(sl + 1) * 256]),
                         reads=[r_vtok[blk]], writes=[r_ln[blk]])
                done_piece()
            for blk in range(NB):
                S.op("dve", lambda e, blk=blk: e.bn_aggr(out=lnmv[:, blk, 0:2],
                                                         in_=lnst[:, blk, :, :].rearrange("p a b -> p (a b)")),
                     reads=[], writes=[r_ln[blk]])
                S.op("act", lambda e, blk=blk: e.activation(out=lnmv[:, blk, 2:3], in_=lnmv[:, blk, 1:2], func=AF.Ln,
                                                            bias=epsc[:, 1:2], scale=1.0),
                     reads=[], writes=[r_ln[blk]])
                S.op("act", lambda e, blk=blk: e.activation(out=lnmv[:, blk, 2:3], in_=lnmv[:, blk, 2:3], func=AF.Exp,
                                                            scale=-0.5),
                     reads=[], writes=[r_ln[blk]])
                S.op("dve", lambda e, blk=blk: e.scalar_tensor_tensor(
                    out=lnmv[:, blk, 3:4], in0=lnmv[:, blk, 0:1], scalar=-1.0, in1=lnmv[:, blk, 2:3],
                    op0=ALU.mult, op1=ALU.mult), reads=[], writes=[r_ln[blk]])
            vn_dst = [[r_vn0], [r_vtok[0]], [r_vtok[0]], [r_vtok[1]]]
            for blk in range(NB):
                S.op("dve", lambda e, blk=blk: e.tensor_scalar(
                    out=vnb(blk), in0=vtok[:, blk, :], scalar1=lnmv[:, blk, 2:3], scalar2=lnmv[:, blk, 3:4],
                    op0=ALU.mult, op1=ALU.add), reads=[r_ln[blk], r_vtok[blk]], writes=vn_dst[blk])

            def spatial_gate(g):
                b = nxt("ps", 6)

                def fs(e):
                    ins = None
                    for blk in range(NB):
                        ins = e.matmul(psb[b][:, blk * 128:(blk + 1) * 128], lhsT=vnb(blk)[:, g * 128:(g + 1) * 128],
                                       rhs=WsT[:, g, :], start=True, stop=True)
                    return ins
                S.op("pe", fs, reads=[r_vn0, r_vtok[0], r_vtok[1]], writes=[r_ps[b]])
                tk = nxt("tmpg", 2)
                S.op("dve", lambda e: e.scalar_tensor_tensor(
                    out=tmpg[:, tk, :].rearrange("p (b t) -> p b t", b=NB),
                    in0=psb[b][:].rearrange("p (b t) -> p b t", b=NB), scalar=lncol[:, 0, g:g + 1],
                    in1=Amat[:, g, :].unsqueeze(1).broadcast_to([128, NB, 128]), op0=ALU.mult, op1=ALU.add),
                    reads=[r_ps[b]], writes=[r_tmpg[tk]])
                S.op("dve", lambda e: e.tensor_tensor(out=uT[:, g, :], in0=uT[:, g, :], in1=tmpg[:, tk, :], op=ALU.mult),
                     reads=[r_tmpg[tk]], writes=[r_q[g]])

            for c in range(KC):
                if c % 2 == 0:
                    s = next_piece()
                wv = wring[:, s, 0:4096].rearrange("p (k m) -> p k m", k=KC)
                m0 = (c % 2) * 128
                b, fn = proj_fm(lambda k, wv=wv, m0=m0: wv[:, k, m0:m0 + 128],
                                lambda k: hT[:, k, HALO:HALO + T], None, KC, T, None)
                S.op("pe", fn, reads=[r_ws[s]] + r_h, writes=[r_ps[b]])
                if c % 2 == 1:
                    done_piece()
                uk = nxt("utmp", 2)
                S.op("dve", lambda e, b=b, uk=uk: e.tensor_tensor(out=utmp[:, uk, :], in0=psb[b][:], in1=rstd[:, rk4, :],
                                                                  op=ALU.mult),
                     reads=[r_ps[b], r_rstd[rk4]], writes=[r_utmp[uk]])
                S.op("act", lambda e, uk=uk, c=c: e.activation(out=uT[:, c, :], in_=utmp[:, uk, :], func=AF.Gelu_apprx_tanh),
                     reads=[r_utmp[uk]], writes=[r_q[c]])
                if c >= 2:
                    spatial_gate(c - 2)
            spatial_gate(KC - 2)
            spatial_gate(KC - 1)
            ssb = out_proj(uT, r_q, KC, None, r_h + r_vn + r_utmp + r_tmpg + [r_vn0])
            postnorm_residual(5, ssb)

        def load_x_chunk(t, c):
            S.op("sp" if t == 0 else "pool", lambda e: e.dma_start(out=xT[:, c, :], in_=x_in[c, :, t * T:(t + 1) * T]),
                 writes=[r_x[c]], sem=("xld%d" if t == 0 else "xlp%d") % c, inc=16)

        last_layer = max(layers)
        for t in range(n_tiles):
            if t == 0:
                for c in range(KC):
                    load_x_chunk(0, c)

            def after_chunk(c, t=t):
                S.op("sp", lambda e: e.dma_start(out=out_d[c, :, t * T:(t + 1) * T], in_=xT[:, c, :]),
                     reads=[r_x[c]], sem="ost%d" % c, inc=16)
                if t + 1 < n_tiles:
                    load_x_chunk(t + 1, c)
            if 0 in layers:
                attn_layer(t)
                ffn(0, 2, 3, after_chunk if last_layer == 0 else None)
            if 1 in layers:
                gmlp_layer(t)
                ffn(1, 6, 7, after_chunk)
        S.final_wait("sp", ["ost%d" % c for c in range(KC)])

        with nc.Block() as block:
            @block.sync
            def _(e):
                S.replay("sp", e)

            @block.gpsimd
            def _(e):
                S.replay("pool", e)

            @block.scalar
            def _(e):
                S.replay("act", e)

            @block.vector
            def _(e):
                S.replay("dve", e)

            @block.tensor
            def _(e):
                S.replay("pe", e)
    return nc


def _t5_bucket(dist):
    max_exact = 16
    d_f = np.maximum(dist, 1).astype(np.float32)
    large = max_exact + (np.log(d_f / np.float32(max_exact)) / np.float32(math.log(128 / max_exact))
                         * np.float32(32 - max_exact)).astype(np.int32)
    large = np.minimum(large, 31)
    return np.where(dist < max_exact, dist, large)


def _colpieces(W):
    K, N = W.shape
    n = N // 256
    return np.ascontiguousarray(W.reshape(KC, 128, n, 256).transpose(2, 1, 0, 3)).reshape(n, 128, KC * 256)


def _prep_shared(inp):
    f = np.float32
    wqkv = inp["attn_w_qkv"][0]
    wo = inp["attn_w_o"][0]
    q_p = _colpieces(wqkv[:, 0:2048])
    kd = np.concatenate([np.concatenate([wqkv[:, 2048 + g * 64:2048 + (g + 1) * 64]] * 2, axis=1) for g in range(4)], axis=1)
    kd_p = _colpieces(kd)
    v_p = _colpieces(wqkv[:, 2304:2560])
    wo_p = _colpieces(wo)

    def gu_pieces(wgu):
        gate = wgu[:, :HC * 128].reshape(D, HC, 128)
        up = wgu[:, HC * 128:].reshape(D, HC, 128)
        inter = np.concatenate([gate, up], axis=2).reshape(D, HC * 256)
        return _colpieces(inter)

    def dn_pieces(wd):
        return np.ascontiguousarray(wd.reshape(HC, 128, KC, 128).transpose(2, 1, 0, 3)).reshape(KC, 128, HC * 128)

    wA0 = np.concatenate([q_p, kd_p, v_p, wo_p, gu_pieces(inp["ffn_w_gate_up"][0])], axis=0)
    wD0 = dn_pieces(inp["ffn_w_down"][0])
    win = inp["gmlp_w_in"][0]
    wA1 = np.concatenate([_colpieces(win[:, 2048:4096]), _colpieces(win[:, 0:2048]),
                          _colpieces(inp["gmlp_w_out"][0]), gu_pieces(inp["ffn_w_gate_up"][1])], axis=0)
    wD1 = dn_pieces(inp["ffn_w_down"][1])
    assert wA0.shape[0] == NA[0] and wA1.shape[0] == NA[1]

    ng = inp["norm_gains"].reshape(8, KC, 128)
    gcol = np.ascontiguousarray(ng.transpose(2, 0, 1)).reshape(128, 8 * KC)
    j = np.arange(128)[:, None]
    q = np.arange(128)[None, :]
    dist = np.where(j <= q, q - j, q + 128 - j).astype(np.int32)
    bucket = _t5_bucket(dist)
    tab = inp["rel_bias_table"]
    hb = tab[bucket]
    hb = hb.reshape(128, 128, 4, 4, 2)
    biasC = np.ascontiguousarray(hb.transpose(0, 2, 4, 3, 1)).reshape(128, 32 * 128)
    sk = inp["attn_sinks"][0].reshape(4, 4, 2).transpose(0, 2, 1).reshape(32)
    sinkb = np.ascontiguousarray(np.broadcast_to(sk[None, :], (128, 32))).astype(f)
    lncol = np.stack([inp["gmlp_ln_gain"][0].reshape(KC, 128).T, inp["gmlp_ln_bias"][0].reshape(KC, 128).T], axis=1)
    lncol = np.ascontiguousarray(lncol).reshape(128, 2 * KC)
    bspb = np.ascontiguousarray(np.broadcast_to(inp["gmlp_b_spatial"][0].reshape(1, KC * 128), (128, KC * 128)))
    wst = np.ascontiguousarray(inp["gmlp_w_spatial"][0].transpose(2, 0, 1)).reshape(128, KC * 128)
    return {"wA0": wA0, "wD0": wD0, "wA1": wA1, "wD1": wD1, "gcol": gcol.astype(f), "biasC": biasC.astype(f),
            "sinkb": sinkb, "lncol": lncol.astype(f), "bspb": bspb.astype(f), "wst": wst.astype(f)}


def run_module(inputs, layers=(0, 1), trace=False):
    x = np.asarray(inputs["x"], dtype=np.float32)
    B, Sq, _ = x.shape
    cps = N_CORES // B
    tpc = Sq // cps
    n_tiles = tpc // T
    shared = _prep_shared({k: np.asarray(v, dtype=np.float32) for k, v in inputs.items() if k != "x"})
    in_maps = []
    for core in range(N_CORES):
        b, part = divmod(core, cps)
        s0 = part * tpc
        m = dict(shared)
        m["x_in"] = np.ascontiguousarray(x[b, s0:s0 + tpc, :].T).reshape(KC, 128, tpc)
        if part == 0:
            m["xh_in"] = np.zeros((KC, 128, HALO), np.float32)
            m["haloneg"] = np.full((128, 1), NEG, np.float32)
        else:
            m["xh_in"] = np.ascontiguousarray(x[b, s0 - HALO:s0, :].T).reshape(KC, 128, HALO)
            m["haloneg"] = np.zeros((128, 1), np.float32)
        in_maps.append(m)
    nc = build_program(n_tiles, layers)
    res = run_bass_kernel_spmd(nc, in_maps, core_ids=list(range(N_CORES)), trace=trace)
    out = np.empty((B, Sq, D), np.float32)
    for core in range(N_CORES):
        b, part = divmod(core, cps)
        s0 = part * tpc
        out[b, s0:s0 + tpc, :] = np.asarray(res.results[core]["out"]).reshape(D, tpc).T
    return out, res


def kernel(**inputs):
    out, _ = run_module(inputs)
    return out
```

```python
import contextlib
import math

import numpy as np

import concourse.bass as bass
import concourse.mybir as mybir
from concourse.bass_utils import run_bass_kernel_spmd

F32 = mybir.dt.float32
BF16 = mybir.dt.bfloat16
AF = mybir.ActivationFunctionType
ALU = mybir.AluOpType

N_CORES = 8
D = 2048
KC = 16
T = 512
NB = 4
HC = 44
HALO = 128
NSLOT = 4
SLOT_E = 5632
NORM_EPS = 1e-6
LN_EPS = 1e-5
NEG = -1e30
SAME_ENGINE_SYNC = True
NCONV = 12
CONV_LOOK = 20

A0_Q, A0_KD, A0_V, A0_WO, A0_GU = 0, 8, 10, 11, 19
A1_U, A1_V, A1_WOUT, A1_GU = 0, 8, 16, 24
NA = (63, 68)


class Res:
    __slots__ = ("name", "w", "r", "guard")

    def __init__(self, name):
        self.name = name
        self.w = None
        self.r = {}
        self.guard = {}


class Sched:
    def __init__(self, nc, stack):
        self.nc = nc
        self.stack = stack
        self.engs = ["pe", "act", "dve", "pool", "sp"]
        self.sems = {}
        self.cnt = {}
        self.waited = {e: {} for e in self.engs}
        self.streams = {e: [] for e in self.engs}
        for e in self.engs:
            self.new_sem(e)

    def new_sem(self, key):
        self.sems[key] = self.stack.enter_context(self.nc.semaphore("s_" + key))
        self.cnt[key] = 0

    def op(self, eng, fn, reads=(), writes=(), sem=None, inc=1):
        deps = {}

        def add(k, v):
            if deps.get(k, 0) < v:
                deps[k] = v

        for r in reads:
            if r.w is not None:
                add(*r.w)
        for w in writes:
            if w.w is not None:
                add(*w.w)
            for k, v in w.r.items():
                add(k, v)
            for k, v in w.guard.items():
                add(k, v)
            w.guard = {}
        if eng == "pe" or not SAME_ENGINE_SYNC:
            deps.pop(eng, None)
        st = self.streams[eng]
        wd = self.waited[eng]
        for k, v in deps.items():
            if wd.get(k, 0) < v:
                wd[k] = v
                st.append(("wait", k, v))
        key = sem or eng
        self.cnt[key] += inc
        val = self.cnt[key]
        st.append(("op", fn, key, inc))
        for r in reads:
            if r.r.get(key, 0) < val:
                r.r[key] = val
        for w in writes:
            w.w = (key, val)
            w.r = {}
        return (key, val)

    def snapshot(self, res_list):
        deps = {}
        for r in res_list:
            if r.w is not None and deps.get(r.w[0], 0) < r.w[1]:
                deps[r.w[0]] = r.w[1]
            for k, v in r.r.items():
                if deps.get(k, 0) < v:
                    deps[k] = v
        return deps

    def guard(self, new_list, old_list):
        deps = self.snapshot(old_list)
        for r in new_list:
            for k, v in deps.items():
                if r.guard.get(k, 0) < v:
                    r.guard[k] = v

    def barrier(self, keys):
        for e in self.engs:
            for k in keys:
                v = self.cnt[k]
                if v > 0 and self.waited[e].get(k, 0) < v and k != e:
                    self.waited[e][k] = v
                    self.streams[e].append(("wait", k, v))

    def final_wait(self, eng, keys):
        for k in keys:
            self.streams[eng].append(("wait", k, self.cnt[k]))

    def replay(self, eng, e):
        for item in self.streams[eng]:
            if item[0] == "wait":
                e.wait_ge(self.sems[item[1]], item[2])
            else:
                _, fn, key, inc = item
                ins = fn(e)
                ins.then_inc(self.sems[key], inc)


def build_program(n_tiles, layers=(0, 1)):
    nc = bass.Bass("TRN2", target_bir_lowering=False)
    tpc = n_tiles * T

    def din(name, shape, dt=F32):
        return nc.dram_tensor(name, list(shape), dt, kind="ExternalInput").ap()

    x_in = din("x_in", [KC, 128, tpc])
    xh_in = din("xh_in", [KC, 128, HALO])
    out_d = nc.dram_tensor("out", [KC, 128, tpc], F32, kind="ExternalOutput").ap()
    wA = [din("wA0", [NA[0], 128, 4096]), din("wA1", [NA[1], 128, 4096])]
    wD = [din("wD0", [16, 128, SLOT_E]), din("wD1", [16, 128, SLOT_E])]
    wAb = [nc.dram_tensor("wA%db" % l, [NA[l], 128, 4096], BF16, kind="Internal").ap() for l in range(2)]
    wDb = [nc.dram_tensor("wD%db" % l, [16, 128, SLOT_E], BF16, kind="Internal").ap() for l in range(2)]
    gcol_d = din("gcol", [128, 8 * KC])
    biasC_d = din("biasC", [128, 32 * 128])
    sinkb_d = din("sinkb", [128, 32])
    haloneg_d = din("haloneg", [128, 1])
    lncol_d = din("lncol", [128, 2 * KC])
    bsp_d = din("bspb", [128, KC * 128])
    wst_d = din("wst", [128, KC * 128])

    with contextlib.ExitStack() as stack:
        S = Sched(nc, stack)

        def sb(name, shape, dt):
            return stack.enter_context(nc.sbuf_tensor("sb_" + name, list(shape), dt))

        def ps(name):
            return stack.enter_context(nc.psum_tensor(name, [128, 512], F32))

        xT = sb("xT", [128, KC, T], F32)
        AY = sb("AY", [128, KC * T], F32)
        Bar = sb("Bar", [128, 12288], F32)
        kdT = sb("kdT", [128, 4, HALO + T], BF16)
        VV = sb("VV", [128, NB + 1, 4, 128], BF16)
        biasC = sb("biasC", [128, 8, 4, 128], F32)
        Amat = sb("Amat", [128, KC, 128], F32)
        WsT = sb("WsT", [128, KC, 128], BF16)
        gcol = sb("gcol", [128, 8, KC], F32)
        lncol = sb("lncol", [128, 2, KC], F32)
        sinkb = sb("sinkb", [128, 32], F32)
        expsink = sb("expsink", [128, 32], F32)
        haloneg = sb("haloneg", [128, 1], F32)
        mmat = sb("mmat", [128, 128], BF16)
        ones_bf = sb("ones_bf", [128, 128], BF16)
        ones_f = sb("ones_f", [128, 128], F32)
        wring = sb("wring", [128, NSLOT, SLOT_E], BF16)
        sqring = sb("sqring", [128, 3, T], BF16)
        rstd = sb("rstd", [128, 2, T], F32)
        lnst = sb("lnst", [128, NB, 8, 6], F32)
        lnmv = sb("lnmv", [128, NB, 4], F32)
        ident = sb("ident", [128, 128], BF16)
        maskT = sb("maskT", [128, 2, T], BF16)
        sinkhl = sb("sinkhl", [128, 32], BF16)
        sinktmp = sb("sinktmp", [128, 2, 32], F32)
        m01 = sb("m01", [128, 2], F32)
        epsc = sb("epsc", [128, 2], F32)
        inv128 = sb("inv128", [128, 1], F32)
        rtok = sb("rtok", [128, NB], F32)
        print("sbuf bytes remaining", nc.sbuf_bytes_remaining)

        AYb = AY[:].bitcast(BF16)
        hT = AYb[:, 0:KC * (HALO + T)].rearrange("p (c n) -> p c n", c=KC)
        Y = AY[:].rearrange("p (c n) -> p c n", c=KC)
        vn = AYb[:, 0:NB * D].rearrange("p (b f) -> p b f", b=NB)
        wsf = AY[:, 0:KC * 128].rearrange("p (g t) -> p g t", g=KC)
        utmp = AY[:, 6144:6144 + 2 * T].rearrange("p (k n) -> p k n", k=2)
        tmpg = AY[:, 5120:5120 + 2 * T].rearrange("p (k n) -> p k n", k=2)
        Bb = Bar[:].bitcast(BF16)
        qT = Bb[:, 0:KC * T].rearrange("p (c n) -> p c n", c=KC)
        tmpr = Bar[:, 4096:4096 + 4 * T].rearrange("p (k n) -> p k n", k=4)
        ptr = Bb[:, 12288:12288 + 4 * T].rearrange("p (k n) -> p k n", k=4)
        denr = Bar[:, 7168:7168 + 2 * T].rearrange("p (k n) -> p k n", k=2)
        xh = Bar[:, 10240:10240 + KC * HALO].rearrange("p (c n) -> p c n", c=KC)
        uT = qT
        vtok = Bar[:, 4096:4096 + NB * D].rearrange("p (b f) -> p b f", b=NB)
        hid = Bb[:, 0:HC * T].rearrange("p (j n) -> p j n", j=HC)

        def vnb(blk):
            if blk == 0:
                return AYb[:, 14336:14336 + D]
            o = 8192 + (blk - 1) * D
            return Bb[:, o:o + D]
        sgr = Bar[:, 11264:11264 + 2 * T].rearrange("p (k n) -> p k n", k=2)

        psb = [ps("ps%d" % i) for i in range(8)]

        def RL(name, n):
            return [Res("%s%d" % (name, i)) for i in range(n)]

        r_x = RL("x", KC)
        r_h = RL("h", KC)
        r_Y = RL("Y", KC)
        r_vn = []
        r_vn0 = Res("vn0")
        r_q = RL("q", KC)
        r_tmp = RL("tmp", 4)
        r_pt = RL("pt", 4)
        r_den = RL("den", 2)
        r_xh = RL("xh", 1)
        r_vtok = RL("vtok", NB)
        r_hid = RL("hid", HC)
        r_sg = RL("sg", 2)
        r_kd = RL("kd", 4)
        r_vv = RL("vv", NB + 1)
        r_ps = RL("ps", 6)
        r_ss = RL("ss", 2)
        r_sq = RL("sq", 3)
        r_rstd = RL("rstd", 2)
        r_ws = RL("ws", NSLOT)
        r_ln = RL("ln", NB)
        r_setup = Res("setup")
        r_tmpg = RL("tmpg", 2)
        r_rtok = Res("rtok")
        r_utmp = RL("utmp", 2)
        r_scrA = [RL("scrA0_", NA[0]), RL("scrA1_", NA[1])]
        r_scrD = [RL("scrD0_", 16), RL("scrD1_", 16)]
        r_conv = RL("conv", NCONV)
        for i in range(NSLOT):
            S.new_sem("wslot%d" % i)
        for i in range(NSLOT):
            S.new_sem("wcast%d" % i)
            S.new_sem("wstore%d" % i)
        S.new_sem("setupdma")
        S.new_sem("xhld")
        for c in range(KC):
            S.new_sem("xld%d" % c)
            S.new_sem("xlp%d" % c)
            S.new_sem("ost%d" % c)

        AY_groups = [r_h, r_Y, r_vn]
        B_attn = r_q + r_tmp + r_pt + r_den + r_xh
        B_gmlp = r_q + r_vtok
        B_ffn = r_hid + r_sg

        state = {"ps": 0, "ss": 0, "sq": 0, "rstd": 0, "tmp": 0, "pt": 0, "den": 0, "sg": 0,
                 "ev": 0, "tmpg": 0, "utmp": 0}

        def nxt(name, n):
            v = state[name]
            state[name] = (v + 1) % n
            return v

        pieces = []
        for t in range(n_tiles):
            for l in range(2):
                if l not in layers:
                    continue
                for i in range(NA[l]):
                    pieces.append((wAb[l][i], 4096, r_scrA[l][i], wA[l][i], t))
                    if i == NA[l] - 1:
                        for c in range(16):
                            pieces.append((wDb[l][c], SLOT_E, r_scrD[l][c], wD[l][c], t))
        wstate = {"next_load": 0, "next_use": 0}
        ppt = len(pieces) // n_tiles

        def issue_load():
            i = wstate["next_load"]
            if i >= len(pieces):
                return
            wstate["next_load"] = i + 1
            scr, ne, res, src32, t = pieces[i]
            s = i % NSLOT
            odd = (i % ppt) % 2 == 1
            cast = (t == 0) or (t == 1 and odd)
            store = ((t == 0 and not odd) or (t == 1 and odd) or (t == 0 and n_tiles == 2)) and t + 1 < n_tiles
            if cast:
                S.op("pool", lambda e: e.dma_start(out=wring[:, s, 0:ne], in_=src32),
                     reads=[], writes=[r_ws[s]], sem="wcast%d" % s, inc=16)
                if store:
                    S.op("sp", lambda e: e.dma_start(out=scr, in_=wring[:, s, 0:ne]),
                         reads=[r_ws[s]], writes=[res], sem="wstore%d" % s, inc=16)
            else:
                S.op("sp", lambda e: e.dma_start(out=wring[:, s, 0:ne], in_=scr),
                     reads=[res], writes=[r_ws[s]], sem="wslot%d" % s, inc=16)

        def next_piece():
            i = wstate["next_use"]
            wstate["next_use"] = i + 1
            return i % NSLOT

        def done_piece():
            issue_load()

        def sdma(dst, src):
            S.op("sp", lambda e: e.dma_start(out=dst, in_=src), writes=[r_setup], sem="setupdma", inc=16)

        sdma(gcol[:].rearrange("p a c -> p (a c)"), gcol_d)
        sdma(biasC[:].rearrange("p a i q -> p (a i q)"), biasC_d)
        sdma(sinkb[:], sinkb_d)
        sdma(haloneg[:], haloneg_d)
        sdma(lncol[:].rearrange("p a c -> p (a c)"), lncol_d)
        sdma(Amat[:].rearrange("p g t -> p (g t)"), bsp_d)
        sdma(AY[:, 0:KC * 128], wst_d)
        r_c = [Res("c%d" % i) for i in range(8)]
        S.op("pool", lambda e: e.memset(mmat[:], 1.0 / D), writes=[r_c[0]])
        S.op("pool", lambda e: e.memset(ones_bf[:], 1.0), writes=[r_c[1]])
        S.op("pool", lambda e: e.memset(ones_f[:], 1.0), writes=[r_c[2]])
        S.op("pool", lambda e: e.memset(epsc[:, 0:1], NORM_EPS), writes=[r_c[7]])
        S.op("pool", lambda e: e.memset(epsc[:, 1:2], LN_EPS), writes=[r_c[7]])
        S.op("pool", lambda e: e.memset(inv128[:], 1.0 / 128), writes=[r_c[7]])
        S.op("pool", lambda e: e.affine_select(out=wsf, in_=wsf, pattern=[[0, KC], [1, 128]],
                                               compare_op=ALU.is_ge, fill=0.0, base=0,
                                               channel_multiplier=-1),
             reads=[r_setup], writes=[r_c[3]])
        S.op("act", lambda e: e.copy(out=WsT[:], in_=wsf), reads=[r_c[3]], writes=[r_c[4]])
        S.op("act", lambda e: e.activation(out=expsink[:], in_=sinkb[:], func=AF.Exp),
             reads=[r_setup], writes=[r_c[5]])
        S.op("pool", lambda e: e.memset(m01[:], 1.0), writes=[r_c[5]])
        S.op("pool", lambda e: e.affine_select(out=m01[:, 0:1], in_=m01[:, 0:1], pattern=[[0, 1]], compare_op=ALU.is_equal,
                                               fill=0.0, base=0, channel_multiplier=1), writes=[r_c[5]])
        S.op("pool", lambda e: e.affine_select(out=m01[:, 1:2], in_=m01[:, 1:2], pattern=[[0, 1]], compare_op=ALU.is_equal,
                                               fill=0.0, base=-1, channel_multiplier=1), writes=[r_c[5]])
        S.op("dve", lambda e: e.tensor_copy(out=sinkhl[:], in_=expsink[:]), reads=[r_c[5]], writes=[r_c[5]])
        S.op("dve", lambda e: e.tensor_copy(out=sinktmp[:, 0, :], in_=sinkhl[:]), reads=[], writes=[r_c[5]])
        S.op("dve", lambda e: e.tensor_tensor(out=sinktmp[:, 1, :], in0=expsink[:], in1=sinktmp[:, 0, :],
                                              op=ALU.subtract), reads=[], writes=[r_c[5]])
        S.op("dve", lambda e: e.tensor_scalar(out=sinktmp[:, 0, :], in0=sinktmp[:, 0, :], scalar1=m01[:, 0:1],
                                              scalar2=None, op0=ALU.mult), reads=[], writes=[r_c[5]])
        S.op("dve", lambda e: e.scalar_tensor_tensor(out=sinktmp[:, 0, :], in0=sinktmp[:, 1, :], scalar=m01[:, 1:2],
                                                     in1=sinktmp[:, 0, :], op0=ALU.mult, op1=ALU.add),
             reads=[], writes=[r_c[5]])
        S.op("dve", lambda e: e.tensor_copy(out=sinkhl[:], in_=sinktmp[:, 0, :]), reads=[], writes=[r_c[5]])
        S.op("pool", lambda e: e.memset(ident[:], 1.0), writes=[r_c[7]])
        S.op("pool", lambda e: e.affine_select(out=ident[:], in_=ident[:], pattern=[[-1, 128]], compare_op=ALU.is_equal,
                                               fill=0.0, base=0, channel_multiplier=1), writes=[r_c[7]])
        S.op("pool", lambda e: e.memset(maskT[:], 0.0), writes=[r_c[7]])
        mp = maskT[:, 0, :].rearrange("p (i q) -> p i q", i=4)
        mc = maskT[:, 1, :].rearrange("p (i q) -> p i q", i=4)
        S.op("pool", lambda e: e.affine_select(out=mp, in_=mp, pattern=[[0, 4], [-1, 128]], compare_op=ALU.is_gt,
                                               fill=NEG, base=0, channel_multiplier=1), writes=[r_c[7]])
        S.op("pool", lambda e: e.affine_select(out=mc, in_=mc, pattern=[[0, 4], [1, 128]], compare_op=ALU.is_ge,
                                               fill=NEG, base=0, channel_multiplier=-1), writes=[r_c[7]])
        for _ in range(NSLOT):
            issue_load()
        for k in range(4):
            S.op("pe", lambda e, k=k: e.matmul(psb[k][:].rearrange("p (a t) -> p a t", a=4), lhsT=ones_f[:], rhs=wsf[:, 4 * k:4 * k + 4, :],
                                               start=True, stop=True),
                 reads=[r_c[2], r_c[3]], writes=[r_ps[k]])
            for gg in range(4):
                g = 4 * k + gg
                S.op("dve", lambda e, k=k, gg=gg, g=g: e.scalar_tensor_tensor(
                    out=Amat[:, g, :], in0=psb[k][:, gg * 128:(gg + 1) * 128], scalar=lncol[:, 1, g:g + 1],
                    in1=Amat[:, g, :], op0=ALU.mult, op1=ALU.add),
                    reads=[r_ps[k], r_setup], writes=[r_c[6]])
        S.barrier(["pe", "act", "dve", "pool", "setupdma"])

        def evac_engine():
            v = state["ev"]
            state["ev"] = v ^ 1
            return "act" if v == 0 else "dve"

        def proj_fm(w_ap_fn, rhs_fn, rhs_res, nk, n, evac, M=128):
            b = nxt("ps", 6)

            def fn(e):
                ins = None
                for k in range(nk):
                    ins = e.matmul(psb[b][0:M, 0:n], lhsT=w_ap_fn(k), rhs=rhs_fn(k),
                                   start=(k == 0), stop=(k == nk - 1))
                return ins
            return b, fn

        def proj_pair_kouter(s, wfn_a, wfn_b):
            ba = nxt("ps", 6)
            bb = nxt("ps", 6)
            for k in range(KC):
                def fn(e, k=k):
                    e.matmul(psb[ba][:], lhsT=wfn_a(k), rhs=hT[:, k, HALO:HALO + T], start=(k == 0), stop=(k == KC - 1))
                    return e.matmul(psb[bb][:], lhsT=wfn_b(k), rhs=hT[:, k, HALO:HALO + T], start=(k == 0),
                                    stop=(k == KC - 1))
                S.op("pe", fn, reads=[r_ws[s], r_h[k]], writes=[r_ps[ba], r_ps[bb]])
            return ba, bb

        def stats_begin():
            return nxt("ss", 2)

        def stats_sq(src_ap, src_res, n=T):
            k = nxt("sq", 3)
            S.op("act", lambda e: e.activation(out=sqring[:, k, 0:n], in_=src_ap, func=AF.Square),
                 reads=src_res, writes=[r_sq[k]])
            return k

        def stats_mm(ssb, k, first, last, n=T):
            S.op("pe", lambda e: e.matmul(psb[6 + ssb][:, 0:n], lhsT=mmat[:], rhs=sqring[:, k, 0:n],
                                          start=first, stop=last),
                 reads=[r_sq[k]], writes=[r_ss[ssb]])

        def stats_add(ssb, src_ap, src_res, first, last, n=T):
            k = stats_sq(src_ap, src_res, n)
            stats_mm(ssb, k, first, last, n)

        def stats_finish(ssb, n=T):
            k = nxt("rstd", 2)
            S.op("act", lambda e: e.activation(out=rstd[:, k, 0:n], in_=psb[6 + ssb][:, 0:n], func=AF.Ln,
                                               bias=epsc[:, 0:1], scale=1.0),
                 reads=[r_ss[ssb]], writes=[r_rstd[k]])
            S.op("act", lambda e: e.activation(out=rstd[:, k, 0:n], in_=rstd[:, k, 0:n], func=AF.Exp, scale=-0.5),
                 reads=[], writes=[r_rstd[k]])
            return k

        def prenorm(gidx, guard_old, tokmajor=False):
            others = [r for r in guard_old if r not in r_Y]
            for c in range(KC):
                j0 = (c * 1280) // 2048
                j1 = ((c + 1) * 1280 - 1) // 2048
                ys = r_Y[j0:j1 + 1] if any(r in r_Y for r in guard_old) else []
                S.guard([r_h[c]], ys + others)
            ssb = stats_begin()
            for c in range(KC):
                stats_add(ssb, xT[:, c, :], [r_x[c]], c == 0, c == KC - 1)
                S.op("act", lambda e, c=c: e.mul(out=hT[:, c, HALO:HALO + T], in_=xT[:, c, :],
                                                 mul=gcol[:, gidx, c:c + 1]),
                     reads=[r_x[c]], writes=[r_h[c]])
            rk = stats_finish(ssb)
            if tokmajor:
                b = nxt("ps", 6)

                def ft(e):
                    ins = None
                    for blk in range(NB):
                        ins = e.matmul(psb[b][:, blk:blk + 1], lhsT=rstd[:, rk, blk * 128:(blk + 1) * 128],
                                       rhs=inv128[:, 0:1], start=True, stop=True)
                    return ins
                S.op("pe", ft, reads=[r_rstd[rk]], writes=[r_ps[b]])
                S.op("act", lambda e: e.copy(out=rtok[:, :], in_=psb[b][:, 0:NB]), reads=[r_ps[b]], writes=[r_rtok])
            return rk

        def y_evac(b, c, gidx):
            k = stats_sq(psb[b][:], [r_ps[b]])
            S.op("act", lambda e: e.mul(out=Y[:, c, :], in_=psb[b][:], mul=gcol[:, gidx, c:c + 1]),
                 reads=[r_ps[b]], writes=[r_Y[c]])
            return k

        def postnorm_residual(gidx, ssb, after_chunk=None):
            rk = stats_finish(ssb)
            for c in range(KC):
                S.op("dve", lambda e, c=c: e.tensor_tensor(out=Y[:, c, :], in0=Y[:, c, :], in1=rstd[:, rk, :],
                                                           op=ALU.mult),
                     reads=[r_rstd[rk]], writes=[r_Y[c]])
                S.op("dve", lambda e, c=c: e.tensor_tensor(out=xT[:, c, :], in0=xT[:, c, :], in1=Y[:, c, :],
                                                           op=ALU.add),
                     reads=[r_Y[c]], writes=[r_x[c]])
                if after_chunk is not None:
                    after_chunk(c)

        def out_proj(rhs_ap, rhs_res, nk, piece_w, guard_old):
            S.guard(r_Y, guard_old)
            ssb = stats_begin()
            pend = None
            for c in range(KC):
                if nk == KC:
                    if c % 2 == 0:
                        s = next_piece()
                    wv = wring[:, s, 0:4096].rearrange("p (k m) -> p k m", k=KC)
                    m0 = (c % 2) * 128
                    wfn = lambda k, wv=wv, m0=m0: wv[:, k, m0:m0 + 128]
                else:
                    s = next_piece()
                    wv = wring[:, s, 0:SLOT_E].rearrange("p (k m) -> p k m", k=HC)
                    wfn = lambda k, wv=wv: wv[:, k, :]
                b, fn = proj_fm(wfn, lambda k: rhs_ap[:, k, :], None, nk, T, None)
                S.op("pe", fn, reads=[r_ws[s]] + rhs_res, writes=[r_ps[b]])
                if nk != KC or c % 2 == 1:
                    done_piece()
                if pend is not None:
                    stats_mm(ssb, pend[0], pend[1] == 0, False)
                pend = (y_evac(b, c, piece_w), c)
            stats_mm(ssb, pend[0], False, True)
            return ssb

        def ffn(l, gpre, gpost, after_chunk=None):
            rk = prenorm(gpre, r_Y + r_vn)
            S.guard(B_ffn, B_attn + B_gmlp)
            for j in range(HC):
                s = next_piece()
                wv = wring[:, s, 0:4096].rearrange("p (k m) -> p k m", k=KC)
                if j == 0:
                    bg, bu = proj_pair_kouter(s, lambda k, wv=wv: wv[:, k, 0:128], lambda k, wv=wv: wv[:, k, 128:256])
                else:
                    bg, fg = proj_fm(lambda k, wv=wv: wv[:, k, 0:128], lambda k: hT[:, k, HALO:HALO + T], None, KC, T, None)
                    S.op("pe", fg, reads=[r_ws[s]] + r_h, writes=[r_ps[bg]])
                    bu, fu = proj_fm(lambda k, wv=wv: wv[:, k, 128:256], lambda k: hT[:, k, HALO:HALO + T], None, KC, T, None)
                    S.op("pe", fu, reads=[r_ws[s]] + r_h, writes=[r_ps[bu]])
                done_piece()
                k = nxt("sg", 2)
                S.op("dve", lambda e, k=k, bg=bg: e.tensor_tensor(out=sgr[:, k, :], in0=psb[bg][:], in1=rstd[:, rk, :],
                                                                  op=ALU.mult),
                     reads=[r_ps[bg], r_rstd[rk]], writes=[r_sg[k]])
                S.op("act", lambda e, k=k: e.activation(out=sgr[:, k, :], in_=sgr[:, k, :], func=AF.Silu),
                     reads=[], writes=[r_sg[k]])
                S.op("dve", lambda e, k=k: e.tensor_tensor(out=sgr[:, k, :], in0=sgr[:, k, :], in1=rstd[:, rk, :],
                                                           op=ALU.mult),
                     reads=[r_rstd[rk]], writes=[r_sg[k]])
                S.op("dve", lambda e, k=k, bu=bu, j=j: e.tensor_tensor(out=hid[:, j, :], in0=psb[bu][:],
                                                                       in1=sgr[:, k, :], op=ALU.mult),
                     reads=[r_ps[bu], r_sg[k]], writes=[r_hid[j]])
            ssb = out_proj(hid, r_hid, HC, gpost, r_h + r_vn)
            postnorm_residual(gpost, ssb, after_chunk)

        def attn_layer(t):
            first = (t == 0)
            rk0 = prenorm(0, r_Y + r_vn, tokmajor=True)
            S.guard(B_attn, B_ffn + B_gmlp)
            if first:
                S.op("sp", lambda e: e.dma_start(out=xh, in_=xh_in.rearrange("c p n -> p c n")),
                     writes=r_xh, sem="xhld", inc=16)
                ssb = stats_begin()
                for c in range(KC):
                    stats_add(ssb, xh[:, c, :], r_xh, c == 0, c == KC - 1, n=HALO)
                rk = stats_finish(ssb, n=HALO)
                for c in range(KC):
                    S.op("dve", lambda e, c=c: e.scalar_tensor_tensor(
                        out=hT[:, c, 0:HALO], in0=xh[:, c, :], scalar=gcol[:, 0, c:c + 1],
                        in1=rstd[:, rk, 0:HALO], op0=ALU.mult, op1=ALU.mult),
                        reads=r_xh + [r_rstd[rk]], writes=[r_h[c]])
            for c in range(KC):
                if c % 2 == 0:
                    s = next_piece()
                wv = wring[:, s, 0:4096].rearrange("p (k m) -> p k m", k=KC)
                m0 = (c % 2) * 128
                if c == 0:
                    b, b_next = proj_pair_kouter(s, lambda k, wv=wv: wv[:, k, 0:128], lambda k, wv=wv: wv[:, k, 128:256])
                elif c == 1:
                    b = b_next
                else:
                    b, fn = proj_fm(lambda k, wv=wv, m0=m0: wv[:, k, m0:m0 + 128],
                                    lambda k: hT[:, k, HALO:HALO + T], None, KC, T, None)
                    S.op("pe", fn, reads=[r_ws[s]] + r_h, writes=[r_ps[b]])
                if c % 2 == 1:
                    done_piece()
                S.op("dve", lambda e, b=b, c=c: e.scalar_tensor_tensor(
                    out=qT[:, c, :], in0=psb[b][:], scalar=0.125, in1=rstd[:, rk0, :], op0=ALU.mult, op1=ALU.mult),
                    reads=[r_ps[b], r_rstd[rk0]], writes=[r_q[c]])
            for g in range(4):
                if g % 2 == 0:
                    s = next_piece()
                wv = wring[:, s, 0:4096].rearrange("p (k m) -> p k m", k=KC)
                m0 = (g % 2) * 128
                b, fn = proj_fm(lambda k, wv=wv, m0=m0: wv[:, k, m0:m0 + 128],
                                lambda k: hT[:, k, HALO:HALO + T], None, KC, T, None)
                S.op("pe", fn, reads=[r_ws[s]] + r_h, writes=[r_ps[b]])
                S.op("dve", lambda e, b=b, g=g: e.tensor_tensor(out=kdT[:, g, HALO:HALO + T], in0=psb[b][:],
                                                                in1=rstd[:, rk0, :], op=ALU.mult),
                     reads=[r_ps[b], r_rstd[rk0]], writes=[r_kd[g]])
                if first:
                    b2, fn2 = proj_fm(lambda k, wv=wv, m0=m0: wv[:, k, m0:m0 + 128],
                                      lambda k: hT[:, k, 0:HALO], None, KC, HALO, None)
                    S.op("pe", fn2, reads=[r_ws[s]] + r_h, writes=[r_ps[b2]])
                    S.op("dve", lambda e, b2=b2, g=g: e.tensor_copy(out=kdT[:, g, 0:HALO], in_=psb[b2][:, 0:HALO]),
                         reads=[r_ps[b2]], writes=[r_kd[g]])
                if g % 2 == 1:
                    done_piece()
            s = next_piece()
            wv = wring[:, s, 0:4096].rearrange("p (k m) -> p k m", k=KC)
            for blk in range(0 if first else 1, NB + 1):
                c0 = blk * 128
                b, fn = proj_fm(lambda k, c0=c0: hT[:, k, c0:c0 + 128], lambda k, wv=wv: wv[:, k, :],
                                None, KC, 256, None)
                S.op("pe", fn, reads=[r_ws[s]] + r_h, writes=[r_ps[b]])
                src = psb[b][:, 0:256].rearrange("p (g d) -> p g d", g=4)
                if blk == 0:
                    S.op("act", lambda e, blk=blk, src=src: e.copy(out=VV[:, blk, :, 0:64], in_=src),
                         reads=[r_ps[b]], writes=[r_vv[blk]])
                    S.op("dve", lambda e, blk=blk, src=src: e.tensor_copy(out=VV[:, blk, :, 64:128], in_=src),
                         reads=[r_ps[b]], writes=[r_vv[blk]])
                else:
                    S.op("act", lambda e, blk=blk, src=src: e.mul(out=VV[:, blk, :, 0:64], in_=src,
                                                                  mul=rtok[:, blk - 1:blk]),
                         reads=[r_ps[b], r_rtok], writes=[r_vv[blk]])
                    S.op("dve", lambda e, blk=blk, src=src: e.tensor_scalar(out=VV[:, blk, :, 64:128], in0=src,
                                                                            scalar1=rtok[:, blk - 1:blk], scalar2=None,
                                                                            op0=ALU.mult),
                         reads=[r_ps[b], r_rtok], writes=[r_vv[blk]])
            done_piece()

            combos = [(blk, g, half) for blk in range(NB) for g in range(4) for half in range(2)]
            pend = None

            def scores(blk, g, half):
                p0 = half * 64
                outs = []
                for kb in range(2):
                    kc0 = (blk + kb) * 128
                    b = nxt("ps", 6)
                    def fsc(e, b=b, kc0=kc0, kb=kb):
                        e.matmul(psb[b][:].rearrange("p (i q) -> p i q", i=4), lhsT=kdT[p0:p0 + 64, g, kc0:kc0 + 128],
                                 rhs=qT[p0:p0 + 64, 4 * g:4 * g + 4, blk * 128:(blk + 1) * 128], start=True, stop=False)
                        return e.matmul(psb[b][:], lhsT=ident[:], rhs=maskT[:, kb, :], start=False, stop=True)
                    S.op("pe", fsc, reads=[r_kd[g]] + r_q[4 * g:4 * g + 4], writes=[r_ps[b]])
                    tk = nxt("tmp", 4)
                    S.op("dve", lambda e, b=b, tk=tk: e.tensor_tensor(
                        out=tmpr[:, tk, :], in0=psb[b][:],
                        in1=biasC[:, 2 * g + half, :, :].rearrange("p i q -> p (i q)"), op=ALU.add),
                        reads=[r_ps[b]], writes=[r_tmp[tk]])
                    pk = nxt("pt", 4)
                    if kb == 0 and first and blk == 0:
                        S.op("act", lambda e, tk=tk, pk=pk: e.activation(out=ptr[:, pk, :], in_=tmpr[:, tk, :],
                                                                         func=AF.Exp, bias=haloneg[:, 0:1]),
                             reads=[r_tmp[tk]], writes=[r_pt[pk]])
                    else:
                        S.op("act", lambda e, tk=tk, pk=pk: e.activation(out=ptr[:, pk, :], in_=tmpr[:, tk, :],
                                                                         func=AF.Exp),
                             reads=[r_tmp[tk]], writes=[r_pt[pk]])
                    outs.append(pk)
                return outs

            def pv_norm(blk, g, half, pks):
                p0 = half * 64
                bo = nxt("ps", 6)

                def fo(e):
                    e.matmul(psb[bo][:], lhsT=VV[:, blk, g, :], rhs=ptr[:, pks[0], :], start=True, stop=False)
                    return e.matmul(psb[bo][:], lhsT=VV[:, blk + 1, g, :], rhs=ptr[:, pks[1], :], start=False, stop=True)
                S.op("pe", fo, reads=[r_vv[blk], r_vv[blk + 1], r_pt[pks[0]], r_pt[pks[1]]], writes=[r_ps[bo]])
                bd = nxt("ps", 6)

                hsl = slice(8 * g + 4 * half, 8 * g + 4 * half + 4)

                def fd(e):
                    e.matmul(psb[bd][:], lhsT=ones_bf[:], rhs=ptr[:, pks[0], :], start=True, stop=False)
                    e.matmul(psb[bd][:], lhsT=ones_bf[:], rhs=ptr[:, pks[1], :], start=False, stop=False)
                    return e.matmul(psb[bd][:].rearrange("p (i q) -> p i q", i=4), lhsT=ones_bf[0:2, :],
                                    rhs=sinkhl[0:2, hsl].unsqueeze(2).broadcast_to([2, 4, 128]), start=False, stop=True)
                S.op("pe", fd, reads=[r_pt[pks[0]], r_pt[pks[1]]], writes=[r_ps[bd]])
                dk = nxt("den", 2)
                S.op("act", lambda e: e.activation(out=denr[p0:p0 + 64, dk, :], in_=psb[bd][p0:p0 + 64, :], func=AF.Ln),
                     reads=[r_ps[bd]], writes=[r_den[dk]])
                S.op("act", lambda e: e.activation(out=denr[p0:p0 + 64, dk, :], in_=denr[p0:p0 + 64, dk, :], func=AF.Exp,
                                                   scale=-1.0),
                     reads=[], writes=[r_den[dk]])
                S.op("dve", lambda e: e.tensor_tensor(
                    out=qT[p0:p0 + 64, 4 * g:4 * g + 4, blk * 128:(blk + 1) * 128],
                    in0=psb[bo][p0:p0 + 64, :].rearrange("p (i q) -> p i q", i=4),
                    in1=denr[p0:p0 + 64, dk, :].rearrange("p (i q) -> p i q", i=4), op=ALU.mult),
                    reads=[r_ps[bo], r_den[dk]], writes=r_q[4 * g:4 * g + 4])

            for (blk, g, half) in combos:
                pks = scores(blk, g, half)
                if pend is not None:
                    pv_norm(*pend)
                pend = (blk, g, half, pks)
            pv_norm(*pend)
            if t + 1 < n_tiles:
                for g in range(4):
                    S.op("pool", lambda e, g=g: e.tensor_copy(out=kdT[:, g, 0:HALO], in_=kdT[:, g, T:T + HALO]),
                         reads=[], writes=[r_kd[g]])
                S.op("pool", lambda e: e.tensor_copy(out=VV[:, 0, :, :], in_=VV[:, NB, :, :]),
                     reads=[r_vv[NB]], writes=[r_vv[0]])
            ssb = out_proj(qT, r_q, KC, 1, r_h + r_vn)
            postnorm_residual(1, ssb)

        def gmlp_layer(t):
            rk4 = prenorm(4, r_Y + r_vn, tokmajor=True)
            S.guard(B_gmlp, B_ffn + B_attn)
            S.guard(r_utmp + r_tmpg + [r_vn0], r_Y)
            for sl in range(8):
                s = next_piece()
                wv = wring[:, s, 0:4096].rearrange("p (k m) -> p k m", k=KC)
                for blk in range(NB):
                    c0 = HALO + blk * 128
                    b, fn = proj_fm(lambda k, c0=c0: hT[:, k, c0:c0 + 128], lambda k, wv=wv: wv[:, k, :],
                                    None, KC, 256, None)
                    S.op("pe", fn, reads=[r_ws[s]] + r_h, writes=[r_ps[b]])
                    S.op("act", lambda e, b=b, blk=blk, sl=sl: e.activation(
                        out=vtok[:, blk, sl * 256:(sl + 1) * 256], in_=psb[b][:, 0:256], func=AF.Gelu_apprx_tanh,
                        scale=rtok[:, blk:blk + 1]),
                        reads=[r_ps[b], r_rtok], writes=[r_vtok[blk]])
                    S.op("dve", lambda e, blk=blk, sl=sl: e.bn_stats(out=lnst[:, blk, sl, :],
                                                                     in_=vtok[:, blk, sl * 256:(sl + 1) * 256]),
                         reads=[r_vtok[blk]], writes=[r_ln[blk]])
                done_piece()
            for blk in range(NB):
                S.op("dve", lambda e, blk=blk: e.bn_aggr(out=lnmv[:, blk, 0:2],
                                                         in_=lnst[:, blk, :, :].rearrange("p a b -> p (a b)")),
                     reads=[], writes=[r_ln[blk]])
                S.op("act", lambda e, blk=blk: e.activation(out=lnmv[:, blk, 2:3], in_=lnmv[:, blk, 1:2], func=AF.Ln,
                                                            bias=epsc[:, 1:2], scale=1.0),
                     reads=[], writes=[r_ln[blk]])
                S.op("act", lambda e, blk=blk: e.activation(out=lnmv[:, blk, 2:3], in_=lnmv[:, blk, 2:3], func=AF.Exp,
                                                            scale=-0.5),
                     reads=[], writes=[r_ln[blk]])
                S.op("dve", lambda e, blk=blk: e.scalar_tensor_tensor(
                    out=lnmv[:, blk, 3:4], in0=lnmv[:, blk, 0:1], scalar=-1.0, in1=lnmv[:, blk, 2:3],
                    op0=ALU.mult, op1=ALU.mult), reads=[], writes=[r_ln[blk]])
            vn_dst = [[r_vn0], [r_vtok[0]], [r_vtok[0]], [r_vtok[1]]]
            for blk in range(NB):
                S.op("dve", lambda e, blk=blk: e.tensor_scalar(
                    out=vnb(blk), in0=vtok[:, blk, :], scalar1=lnmv[:, blk, 2:3], scalar2=lnmv[:, blk, 3:4],
                    op0=ALU.mult, op1=ALU.add), reads=[r_ln[blk], r_vtok[blk]], writes=vn_dst[blk])

            def spatial_gate(g):
                b = nxt("ps", 6)

                def fs(e):
                    ins = None
                    for blk in range(NB):
                        ins = e.matmul(psb[b][:, blk * 128:(blk + 1) * 128], lhsT=vnb(blk)[:, g * 128:(g + 1) * 128],
                                       rhs=WsT[:, g, :], start=True, stop=True)
                    return ins
                S.op("pe", fs, reads=[r_vn0, r_vtok[0], r_vtok[1]], writes=[r_ps[b]])
                tk = nxt("tmpg", 2)
                S.op("dve", lambda e: e.scalar_tensor_tensor(
                    out=tmpg[:, tk, :].rearrange("p (b t) -> p b t", b=NB),
                    in0=psb[b][:].rearrange("p (b t) -> p b t", b=NB), scalar=lncol[:, 0, g:g + 1],
                    in1=Amat[:, g, :].unsqueeze(1).broadcast_to([128, NB, 128]), op0=ALU.mult, op1=ALU.add),
                    reads=[r_ps[b]], writes=[r_tmpg[tk]])
                S.op("dve", lambda e: e.tensor_tensor(out=uT[:, g, :], in0=uT[:, g, :], in1=tmpg[:, tk, :], op=ALU.mult),
                     reads=[r_tmpg[tk]], writes=[r_q[g]])

            for c in range(KC):
                if c % 2 == 0:
                    s = next_piece()
                wv = wring[:, s, 0:4096].rearrange("p (k m) -> p k m", k=KC)
                m0 = (c % 2) * 128
                b, fn = proj_fm(lambda k, wv=wv, m0=m0: wv[:, k, m0:m0 + 128],
                                lambda k: hT[:, k, HALO:HALO + T], None, KC, T, None)
                S.op("pe", fn, reads=[r_ws[s]] + r_h, writes=[r_ps[b]])
                if c % 2 == 1:
                    done_piece()
                uk = nxt("utmp", 2)
                S.op("dve", lambda e, b=b, uk=uk: e.tensor_tensor(out=utmp[:, uk, :], in0=psb[b][:], in1=rstd[:, rk4, :],
                                                                  op=ALU.mult),
                     reads=[r_ps[b], r_rstd[rk4]], writes=[r_utmp[uk]])
                S.op("act", lambda e, uk=uk, c=c: e.activation(out=uT[:, c, :], in_=utmp[:, uk, :], func=AF.Gelu_apprx_tanh),
                     reads=[r_utmp[uk]], writes=[r_q[c]])
                if c >= 2:
                    spatial_gate(c - 2)
            spatial_gate(KC - 2)
            spatial_gate(KC - 1)
            ssb = out_proj(uT, r_q, KC, 5, r_h + r_vn + r_utmp + r_tmpg + [r_vn0])
            postnorm_residual(5, ssb)

        def load_x_chunk(t, c):
            S.op("sp" if t == 0 else "pool", lambda e: e.dma_start(out=xT[:, c, :], in_=x_in[c, :, t * T:(t + 1) * T]),
                 writes=[r_x[c]], sem=("xld%d" if t == 0 else "xlp%d") % c, inc=16)

        last_layer = max(layers)
        for t in range(n_tiles):
            if t == 0:
                for c in range(KC):
                    load_x_chunk(0, c)

            def after_chunk(c, t=t):
                S.op("sp", lambda e: e.dma_start(out=out_d[c, :, t * T:(t + 1) * T], in_=xT[:, c, :]),
                     reads=[r_x[c]], sem="ost%d" % c, inc=16)
                if t + 1 < n_tiles:
                    load_x_chunk(t + 1, c)
            if 0 in layers:
                attn_layer(t)
                ffn(0, 2, 3, after_chunk if last_layer == 0 else None)
            if 1 in layers:
                gmlp_layer(t)
                ffn(1, 6, 7, after_chunk)
        S.final_wait("sp", ["ost%d" % c for c in range(KC)])

        with nc.Block() as block:
            @block.sync
            def _(e):
                S.replay("sp", e)

            @block.gpsimd
            def _(e):
                S.replay("pool", e)

            @block.scalar
            def _(e):
                S.replay("act", e)

            @block.vector
            def _(e):
                S.replay("dve", e)

            @block.tensor
            def _(e):
                S.replay("pe", e)
    return nc


def _t5_bucket(dist):
    max_exact = 16
    d_f = np.maximum(dist, 1).astype(np.float32)
    large = max_exact + (np.log(d_f / np.float32(max_exact)) / np.float32(math.log(128 / max_exact))
                         * np.float32(32 - max_exact)).astype(np.int32)
    large = np.minimum(large, 31)
    return np.where(dist < max_exact, dist, large)


def _colpieces(W):
    K, N = W.shape
    n = N // 256
    return np.ascontiguousarray(W.reshape(KC, 128, n, 256).transpose(2, 1, 0, 3)).reshape(n, 128, KC * 256)


def _prep_shared(inp):
    f = np.float32
    wqkv = inp["attn_w_qkv"][0]
    wo = inp["attn_w_o"][0]
    q_p = _colpieces(wqkv[:, 0:2048])
    kd = np.concatenate([np.concatenate([wqkv[:, 2048 + g * 64:2048 + (g + 1) * 64]] * 2, axis=1) for g in range(4)], axis=1)
    kd_p = _colpieces(kd)
    v_p = _colpieces(wqkv[:, 2304:2560])
    wo_p = _colpieces(wo)

    def gu_pieces(wgu):
        gate = wgu[:, :HC * 128].reshape(D, HC, 128)
        up = wgu[:, HC * 128:].reshape(D, HC, 128)
        inter = np.concatenate([gate, up], axis=2).reshape(D, HC * 256)
        return _colpieces(inter)

    def dn_pieces(wd):
        return np.ascontiguousarray(wd.reshape(HC, 128, KC, 128).transpose(2, 1, 0, 3)).reshape(KC, 128, HC * 128)

    wA0 = np.concatenate([q_p, kd_p, v_p, wo_p, gu_pieces(inp["ffn_w_gate_up"][0])], axis=0)
    wD0 = dn_pieces(inp["ffn_w_down"][0])
    win = inp["gmlp_w_in"][0]
    wA1 = np.concatenate([_colpieces(win[:, 2048:4096]), _colpieces(win[:, 0:2048]),
                          _colpieces(inp["gmlp_w_out"][0]), gu_pieces(inp["ffn_w_gate_up"][1])], axis=0)
    wD1 = dn_pieces(inp["ffn_w_down"][1])
    assert wA0.shape[0] == NA[0] and wA1.shape[0] == NA[1]

    ng = inp["norm_gains"].reshape(8, KC, 128)
    gcol = np.ascontiguousarray(ng.transpose(2, 0, 1)).reshape(128, 8 * KC)
    j = np.arange(128)[:, None]
    q = np.arange(128)[None, :]
    dist = np.where(j <= q, q - j, q + 128 - j).astype(np.int32)
    bucket = _t5_bucket(dist)
    tab = inp["rel_bias_table"]
    hb = tab[bucket]
    hb = hb.reshape(128, 128, 4, 4, 2)
    biasC = np.ascontiguousarray(hb.transpose(0, 2, 4, 3, 1)).reshape(128, 32 * 128)
    sk = inp["attn_sinks"][0].reshape(4, 4, 2).transpose(0, 2, 1).reshape(32)
    sinkb = np.ascontiguousarray(np.broadcast_to(sk[None, :], (128, 32))).astype(f)
    lncol = np.stack([inp["gmlp_ln_gain"][0].reshape(KC, 128).T, inp["gmlp_ln_bias"][0].reshape(KC, 128).T], axis=1)
    lncol = np.ascontiguousarray(lncol).reshape(128, 2 * KC)
    bspb = np.ascontiguousarray(np.broadcast_to(inp["gmlp_b_spatial"][0].reshape(1, KC * 128), (128, KC * 128)))
    wst = np.ascontiguousarray(inp["gmlp_w_spatial"][0].transpose(2, 0, 1)).reshape(128, KC * 128)
    return {"wA0": wA0, "wD0": wD0, "wA1": wA1, "wD1": wD1, "gcol": gcol.astype(f), "biasC": biasC.astype(f),
            "sinkb": sinkb, "lncol": lncol.astype(f), "bspb": bspb.astype(f), "wst": wst.astype(f)}


def run_module(inputs, layers=(0, 1), trace=False):
    x = np.asarray(inputs["x"], dtype=np.float32)
    B, Sq, _ = x.shape
    cps = N_CORES // B
    tpc = Sq // cps
    n_tiles = tpc // T
    shared = _prep_shared({k: np.asarray(v, dtype=np.float32) for k, v in inputs.items() if k != "x"})
    in_maps = []
    for core in range(N_CORES):
        b, part = divmod(core, cps)
        s0 = part * tpc
        m = dict(shared)
        m["x_in"] = np.ascontiguousarray(x[b, s0:s0 + tpc, :].T).reshape(KC, 128, tpc)
        if part == 0:
            m["xh_in"] = np.zeros((KC, 128, HALO), np.float32)
            m["haloneg"] = np.full((128, 1), NEG, np.float32)
        else:
            m["xh_in"] = np.ascontiguousarray(x[b, s0 - HALO:s0, :].T).reshape(KC, 128, HALO)
            m["haloneg"] = np.zeros((128, 1), np.float32)
        in_maps.append(m)
    nc = build_program(n_tiles, layers)
    res = run_bass_kernel_spmd(nc, in_maps, core_ids=list(range(N_CORES)), trace=trace)
    out = np.empty((B, Sq, D), np.float32)
    for core in range(N_CORES):
        b, part = divmod(core, cps)
        s0 = part * tpc
        out[b, s0:s0 + tpc, :] = np.asarray(res.results[core]["out"]).reshape(D, tpc).T
    return out, res


def kernel(**inputs):
    out, _ = run_module(inputs)
    return out
```
